# Optimizing a Trainium2 kernel written in Bass

```python
import jax, jax.numpy as jnp
from jax import lax
import numpy as np

D_MODEL = 2048
BATCH = 2
SEQ = 16384
DEPTH = 1
DEC_BATCH = 8
DEC_SEQ = 32
PAST_LEN = 2048

CHUNK = 64
M_HEADS = 4
M_DQK = 256
M_DV = 256
M_WIDTH = M_HEADS * M_DV
F_HEADS = 8
F_HEAD_DIM = 128
F_WIDTH = F_HEADS * F_HEAD_DIM
Q_BLOCK = 128
D_FF = 4 * D_MODEL
EPS = 1e-6
IN_SPLITS = (M_HEADS * M_DQK, M_HEADS * M_DQK, M_WIDTH, M_WIDTH, M_HEADS, M_HEADS,
             F_WIDTH, F_WIDTH, F_WIDTH, F_HEADS, D_MODEL, D_MODEL)
D_IN = sum(IN_SPLITS)

kernel_name = 'hybrid_mlstm_fox_streaming_step'


def rmsnorm(x, g):
    x32 = x.astype(jnp.float32)
    y = x32 * lax.rsqrt(jnp.mean(x32 * x32, axis=-1, keepdims=True) + EPS)
    return (y * g.astype(jnp.float32)).astype(x.dtype)


def split_in(z):
    offs = np.cumsum(IN_SPLITS)[:-1].tolist()
    return jnp.split(z, offs, axis=-1)


def mlstm_chunk(carry, q, k, v, ig, lf):
    c0, n0, m0 = carry
    L = q.shape[1]
    b = jnp.cumsum(lf, axis=1).transpose(0, 2, 1)
    ih = ig.transpose(0, 2, 1)
    causal = jnp.tril(jnp.ones((L, L), dtype=bool))
    log_d = jnp.where(causal, b[..., :, None] - b[..., None, :] + ih[..., None, :], -jnp.inf)
    log_inter = b + m0[..., None]
    m = jnp.maximum(log_inter, jnp.max(log_d, axis=-1))
    w_intra = jnp.exp(log_d - m[..., None])
    w_inter = jnp.exp(log_inter - m)
    s = jnp.einsum('bthd,bshd->bhts', q, k) * w_intra
    num = (jnp.einsum('bhts,bshe->bthe', s, v)
           + jnp.einsum('bthd,bhde->bthe', q, c0) * w_inter.transpose(0, 2, 1)[..., None])
    den = jnp.sum(s, axis=-1) + w_inter * jnp.einsum('bthd,bhd->bht', q, n0)
    den = jnp.maximum(jnp.abs(den), jnp.exp(-m))
    h = num / den.transpose(0, 2, 1)[..., None]
    m_end = m[..., -1]
    w_state = jnp.exp(b[..., -1] + m0 - m_end)
    w_tok = jnp.exp(b[..., -1:] - b + ih - m_end[..., None])
    c = w_state[..., None, None] * c0 + jnp.einsum('bhs,bshd,bshe->bhde', w_tok, k, v)
    n = w_state[..., None] * n0 + jnp.einsum('bhs,bshd->bhd', w_tok, k)
    return (c, n, m_end), h


def mlstm_run(state, q, k, v, ig, lf):
    B, L = q.shape[:2]
    if L <= CHUNK:
        return mlstm_chunk(state, q, k, v, ig, lf)
    nc = L // CHUNK

    def to_blocks(a):
        return jnp.moveaxis(a.reshape((B, nc, CHUNK) + a.shape[2:]), 1, 0)

    def step(carry, xs):
        return mlstm_chunk(carry, *xs)

    state, h = lax.scan(step, state, (to_blocks(q), to_blocks(k), to_blocks(v), to_blocks(ig), to_blocks(lf)))
    h = jnp.moveaxis(h, 0, 1).reshape((B, L) + h.shape[3:])
    return state, h


def mlstm_mixer(mq, mk, mv, mo, mi, mf, b_i, b_f, norm_h, state):
    B, L = mq.shape[:2]
    f32 = jnp.float32
    q = mq.reshape(B, L, M_HEADS, M_DQK).astype(f32)
    k = mk.reshape(B, L, M_HEADS, M_DQK).astype(f32) * (M_DQK ** -0.5)
    v = mv.reshape(B, L, M_HEADS, M_DV).astype(f32)
    ig = mi.astype(f32) + b_i.astype(f32)
    lf = jax.nn.log_sigmoid(mf.astype(f32) + b_f.astype(f32))
    c0, n0, m0 = state
    new_state, h = mlstm_run((c0.astype(f32), n0.astype(f32), m0.astype(f32)), q, k, v, ig, lf)
    h = h * lax.rsqrt(jnp.mean(h * h, axis=-1, keepdims=True) + EPS)
    h = h.reshape(B, L, M_WIDTH) * norm_h.astype(f32) * jax.nn.sigmoid(mo.astype(f32))
    return h.astype(mq.dtype), new_state


def fox_attend(q, k, v, f_q, f_k, q_pos, k_pos):
    s = jnp.einsum('bqhd,bkhd->bhqk', q, k) * (F_HEAD_DIM ** -0.5)
    s = s + f_q.transpose(0, 2, 1)[..., :, None] - f_k.transpose(0, 2, 1)[..., None, :]
    s = jnp.where(k_pos[None, :] <= q_pos[:, None], s, -jnp.inf)
    p = jax.nn.softmax(s, axis=-1)
    return jnp.einsum('bhqk,bkhd->bqhd', p, v)


def fox_mixer(fq, fk, fv, ff, b_f, past):
    B, L = fq.shape[:2]
    f32 = jnp.float32
    q = fq.reshape(B, L, F_HEADS, F_HEAD_DIM)
    k = fk.reshape(B, L, F_HEADS, F_HEAD_DIM)
    v = fv.reshape(B, L, F_HEADS, F_HEAD_DIM)
    logf = jax.nn.log_sigmoid(ff.astype(f32) + b_f.astype(f32))
    if past is None:
        cum = jnp.cumsum(logf, axis=1)
        nb = L // Q_BLOCK
        k32, v32 = k.astype(f32), v.astype(f32)
        q_blocks = jnp.moveaxis(q.astype(f32).reshape(B, nb, Q_BLOCK, F_HEADS, F_HEAD_DIM), 1, 0)
        f_blocks = jnp.moveaxis(cum.reshape(B, nb, Q_BLOCK, F_HEADS), 1, 0)
        pos_blocks = jnp.arange(L).reshape(nb, Q_BLOCK)
        key_pos = jnp.arange(L)

        def block(args):
            qi, fi, pi = args
            return fox_attend(qi, k32, v32, fi, cum, pi, key_pos)

        out = lax.map(block, (q_blocks, f_blocks, pos_blocks))
        out = jnp.moveaxis(out, 0, 1).reshape(B, L, F_WIDTH)
    else:
        ck, cv, clf = past
        P = ck.shape[1]
        k_all = jnp.concatenate([ck.astype(f32), k.astype(f32)], axis=1)
        v_all = jnp.concatenate([cv.astype(f32), v.astype(f32)], axis=1)
        cum = jnp.cumsum(jnp.concatenate([clf.astype(f32), logf], axis=1), axis=1)
        out = fox_attend(q.astype(f32), k_all, v_all, cum[:, P:], cum,
                         P + jnp.arange(L), jnp.arange(P + L)).reshape(B, L, F_WIDTH)
    return out.astype(fq.dtype), k, v, logf


def trunk_layer(x, mstate, fox_past, norm_mix, w_in, b_mlstm_i, b_mlstm_f, b_fox_f, norm_mlstm_h,
                w_branch_a, w_branch_b, w_out, norm_ffn, w_up, w_down):
    xn = rmsnorm(x, norm_mix)
    z = jnp.einsum('bld,de->ble', xn, w_in)
    mq, mk, mv, mo, mi, mf, fq, fk, fv, ff, ga, gb = split_in(z)
    h_a, (c_new, n_new, m_new) = mlstm_mixer(mq, mk, mv, mo, mi, mf, b_mlstm_i, b_mlstm_f, norm_mlstm_h, mstate)
    h_b, k_rows, v_rows, logf_rows = fox_mixer(fq, fk, fv, ff, b_fox_f, fox_past)
    merged = (jax.nn.sigmoid(ga) * jnp.einsum('blc,cd->bld', h_a, w_branch_a)
              + jax.nn.sigmoid(gb) * jnp.einsum('blc,cd->bld', h_b, w_branch_b))
    x = x + jnp.einsum('bld,de->ble', merged, w_out)
    hn = rmsnorm(x, norm_ffn)
    u = jax.nn.relu(jnp.einsum('bld,df->blf', hn, w_up))
    x = x + jnp.einsum('blf,fd->bld', u * u, w_down)
    return x, (k_rows, v_rows, logf_rows, c_new, n_new, m_new)


def setup_inputs(seed: int = 0) -> dict:
    key = jax.random.key(seed)
    ks = jax.random.split(key, 24)
    f32 = jnp.float32

    def nrm(k, shape, scale):
        return jax.random.normal(k, shape, f32) * scale

    return {
        'x_prompt': nrm(ks[0], (BATCH, SEQ, D_MODEL), 1.0),
        'x_sample': nrm(ks[1], (DEC_BATCH, DEC_SEQ, D_MODEL), 1.0),
        'cache_fox_k': nrm(ks[2], (DEPTH, DEC_BATCH, PAST_LEN, F_HEADS, F_HEAD_DIM), 1.0),
        'cache_fox_v': nrm(ks[3], (DEPTH, DEC_BATCH, PAST_LEN, F_HEADS, F_HEAD_DIM), 1.0),
        'cache_fox_logf': jax.nn.log_sigmoid(3.0 + nrm(ks[4], (DEPTH, DEC_BATCH, PAST_LEN, F_HEADS), 1.0)),
        'state_mlstm_c': nrm(ks[5], (DEPTH, DEC_BATCH, M_HEADS, M_DQK, M_DV), 0.05),
        'state_mlstm_n': nrm(ks[6], (DEPTH, DEC_BATCH, M_HEADS, M_DQK), 0.05),
        'state_mlstm_m': nrm(ks[7], (DEPTH, DEC_BATCH, M_HEADS), 1.0),
        'norm_mix': 1.0 + nrm(ks[8], (DEPTH, D_MODEL), 0.02),
        'w_in': nrm(ks[9], (DEPTH, D_MODEL, D_IN), D_MODEL ** -0.5),
        'b_mlstm_i': nrm(ks[10], (DEPTH, M_HEADS), 0.1),
        'b_mlstm_f': 3.0 + nrm(ks[11], (DEPTH, M_HEADS), 0.1),
        'b_fox_f': 3.0 + nrm(ks[12], (DEPTH, F_HEADS), 0.1),
        'norm_mlstm_h': 1.0 + nrm(ks[13], (DEPTH, M_WIDTH), 0.02),
        'w_branch_a': nrm(ks[14], (DEPTH, M_WIDTH, D_MODEL), M_WIDTH ** -0.5),
        'w_branch_b': nrm(ks[15], (DEPTH, F_WIDTH, D_MODEL), F_WIDTH ** -0.5),
        'w_out': nrm(ks[16], (DEPTH, D_MODEL, D_MODEL), D_MODEL ** -0.5),
        'norm_ffn': 1.0 + nrm(ks[17], (DEPTH, D_MODEL), 0.02),
        'w_up': nrm(ks[18], (DEPTH, D_MODEL, D_FF), D_MODEL ** -0.5),
        'w_down': nrm(ks[19], (DEPTH, D_FF, D_MODEL), D_FF ** -0.5),
        'norm_final': 1.0 + nrm(ks[20], (D_MODEL,), 0.02),
    }


def reference(x_prompt, x_sample, cache_fox_k, cache_fox_v, cache_fox_logf, state_mlstm_c, state_mlstm_n,
              state_mlstm_m, norm_mix, w_in, b_mlstm_i, b_mlstm_f, b_fox_f, norm_mlstm_h, w_branch_a,
              w_branch_b, w_out, norm_ffn, w_up, w_down, norm_final):
    f32 = jnp.float32
    hp, hs = x_prompt, x_sample
    pk, pv, plf, pc, pn, pm = [], [], [], [], [], []
    sk, sv, slf, sc, sn, sm = [], [], [], [], [], []
    for l in range(DEPTH):
        w = (norm_mix[l], w_in[l], b_mlstm_i[l], b_mlstm_f[l], b_fox_f[l], norm_mlstm_h[l],
             w_branch_a[l], w_branch_b[l], w_out[l], norm_ffn[l], w_up[l], w_down[l])
        bp = hp.shape[0]
        fresh = (jnp.zeros((bp, M_HEADS, M_DQK, M_DV), f32), jnp.zeros((bp, M_HEADS, M_DQK), f32),
                 jnp.zeros((bp, M_HEADS), f32))
        hp, st_p = trunk_layer(hp, fresh, None, *w)
        hs, st_s = trunk_layer(hs, (state_mlstm_c[l], state_mlstm_n[l], state_mlstm_m[l]),
                               (cache_fox_k[l], cache_fox_v[l], cache_fox_logf[l]), *w)
        for lst, a in zip((pk, pv, plf, pc, pn, pm), st_p):
            lst.append(a)
        for lst, a in zip((sk, sv, slf, sc, sn, sm), st_s):
            lst.append(a)
    y_prompt = rmsnorm(hp, norm_final)
    y_sample = rmsnorm(hs, norm_final)
    return (y_prompt, y_sample,
            jnp.stack(pk), jnp.stack(pv), jnp.stack(plf), jnp.stack(pc), jnp.stack(pn), jnp.stack(pm),
            jnp.stack(sk), jnp.stack(sv), jnp.stack(slf), jnp.stack(sc), jnp.stack(sn), jnp.stack(sm))
```

```python
import math
import os
from contextlib import ExitStack

import numpy as np
import concourse.bass as bass
import concourse.mybir as mybir
from concourse.bass_utils import run_bass_kernel_spmd

F32 = mybir.dt.float32
BF16 = mybir.dt.bfloat16
AF = mybir.ActivationFunctionType
ALU = mybir.AluOpType
AX = mybir.AxisListType

EPS = 1e-6
MH, MDK, MDV = 4, 256, 256
FH, FD = 8, 128
MW = MH * MDV
FW = FH * FD
NMIX = 4 * MW + 2 * MH + 3 * FW + FH


class Cfg:
    def __init__(s, D=2048, S=16384, DFF=8192, PAST=2048, DEC=32):
        s.D, s.S, s.DFF, s.PAST, s.DEC = D, S, DFF, PAST, DEC
        s.KC = D // 128
        s.Q = S // 4
        s.NT1 = S // 512
        s.NT2 = s.Q // 512
        s.NB = S // 128
        s.DG = min(512, D)
        s.NG = D // s.DG
        s.FCH = DFF // 128
        s.FS = min(16, s.FCH)
        s.NFS = s.FCH // s.FS
        s.NUG = DFF // 512
        s.NSS = 15
        s.oG = 0
        s.oO = s.oG + s.KC
        s.oU = s.oO + s.NG
        s.oD = s.oU + s.NUG
        s.oS = s.oD + s.NG * s.NFS
        s.NSLAB = s.oS
        s.SPC = (s.NSLAB + 7) // 8
        s.SLABW = max((2 * s.KC + 16) * 128, s.KC * 512, s.FS * s.DG)
        s.D_IN = NMIX + 2 * D
        s.PB = PAST // 128


FULL = Cfg()


class Op:
    __slots__ = ("eng", "fn", "deps", "signal", "sem", "val", "dma", "seq")

    def __init__(self, eng, fn, dma):
        self.eng, self.fn, self.dma = eng, fn, dma
        self.deps = []
        self.signal = False
        self.sem = None
        self.val = None
        self.seq = 0


ENGS = ("pe", "act", "dve", "pool", "sp")


class Prog:
    NDMA = {"sp": 10, "pool": 4, "act": 2}

    def __init__(self, nc, stack):
        self.nc = nc
        self.esem = {e: stack.enter_context(nc.semaphore("es_" + e)) for e in ENGS}
        self.dsem = {q: [stack.enter_context(nc.semaphore(f"ds_{q}{i}")) for i in range(n)]
                     for q, n in self.NDMA.items()}
        self.ccsem = stack.enter_context(nc.semaphore("ccs"))
        self.ncc = 0
        self.ecount = {e: 0 for e in ENGS}
        self.dcount = {q: 0 for q in self.NDMA}
        self.dlast = {q: [None] * n for q, n in self.NDMA.items()}
        self.last_w = {}
        self.readers = {}
        self.waited = {e: {} for e in ENGS}
        self.ops = {e: [] for e in ENGS}
        self.all_dma = []
        self.seq = 0

    def add(self, eng, fn, r=(), w=(), dma=False, cc=False):
        op = Op(eng, fn, dma or cc)
        deps = []
        for k in r:
            lw = self.last_w.get(k)
            if lw is not None:
                deps.append(lw)
        for k in w:
            lw = self.last_w.get(k)
            if lw is not None:
                deps.append(lw)
            deps.extend(self.readers.get(k, ()))
        if dma:
            i = self.dcount[eng] % self.NDMA[eng]
            prev = self.dlast[eng][i]
            if prev is not None:
                deps.append(prev)
            op.sem = self.dsem[eng][i]
            op.val = 16 * (self.dcount[eng] // self.NDMA[eng] + 1)
            self.dcount[eng] += 1
            self.dlast[eng][i] = op
            self.all_dma.append(op)
        elif cc:
            self.ncc += 1
            op.sem = self.ccsem
            op.val = self.ncc
            self.all_dma.append(op)
        self.seq += 1
        op.seq = self.seq
        best = {}
        for d in deps:
            if d is op:
                continue
            if eng == "pe" and d.eng == "pe" and not d.dma:
                continue
            k = id(d.sem) if d.dma else d.eng
            if k not in best or best[k].seq < d.seq:
                best[k] = d
        for d in best.values():
            if not d.dma:
                d.signal = True
            op.deps.append(d)
        for k in r:
            self.readers.setdefault(k, []).append(op)
        for k in w:
            self.last_w[k] = op
            self.readers[k] = []
        self.ops[eng].append(op)
        return op

    def barrier(self, keep_cc=False):
        lasts = []
        for e in ENGS:
            for o in reversed(self.ops[e]):
                if not o.dma and o.fn is not None:
                    o.signal = True
                    lasts.append(o)
                    break
        best = {}
        kept = []
        for o in self.all_dma:
            if keep_cc and o.sem is self.ccsem:
                kept.append(o)
                continue
            k = id(o.sem)
            if k not in best or best[k].val < o.val:
                best[k] = o
        dm = list(best.values())
        self.all_dma = kept
        for e in ENGS:
            op = Op(e, None, False)
            op.deps = list(lasts) + dm
            self.ops[e].append(op)
        self.last_w = {}
        self.readers = {}

    def emit(self, block):
        nc = self.nc
        for e in ENGS:
            for o in self.ops[e]:
                if o.dma:
                    continue
                if o.signal:
                    self.ecount[e] += 1
                    o.sem = self.esem[e]
                    o.val = self.ecount[e]
        prog = self

        def body(ename):
            def f(eng):
                waited = prog.waited[ename]
                for o in prog.ops[ename]:
                    for d in o.deps:
                        if waited.get(id(d.sem), -1) >= d.val:
                            continue
                        eng.wait_ge(d.sem, d.val)
                        waited[id(d.sem)] = d.val
                    if o.fn is None:
                        continue
                    ins = o.fn(eng)
                    if o.dma:
                        if o.sem is prog.ccsem:
                            ins.then_inc(o.sem)
                        else:
                            ins.then_inc(o.sem, 16)
                    elif o.signal:
                        ins.then_inc(o.sem, 1)
                prog.ops[ename] = []
            return f

        block.tensor(body("pe"))
        block.scalar(body("act"))
        block.vector(body("dve"))
        block.gpsimd(body("pool"))
        block.sync(body("sp"))


def build_program(cfg):
    c = cfg
    D, S, KC, DG, NG = c.D, c.S, c.KC, c.DG, c.NG
    nc = bass.Bass("TRN2", target_bir_lowering=False)
    dt = nc.dram_tensor

    def din(name, shape, dtype=F32):
        return dt(name, list(shape), dtype, kind="ExternalInput").ap()

    def dout(name, shape, dtype=F32):
        return dt(name, list(shape), dtype, kind="ExternalOutput").ap()

    def dscr(name, shape, dtype=BF16):
        return dt(name, list(shape), dtype, kind="Internal").ap()

    x_b = din("x_b", [S, D])
    x_q = din("x_q", [c.Q, D])
    x_s = din("x_s", [c.DEC, D])
    w1 = din("w1", [D, 2308])
    w2s = din("w2s", [c.SPC * 128, c.SLABW])
    wss = din("wss", [c.NSS * 128, c.KC * 512])
    bias4 = din("bias4", [1, 4])
    bias_s = din("bias_s", [1, 16])
    nmix = din("nmix", [1, D])
    nffn = din("nffn", [1, D])
    nfin = din("nfin", [1, D])
    normh_c = din("normh_c", [128, 2])
    normh_s = din("normh_s", [128, 8])
    ck_s = din("ck_s", [c.PAST, FH * FD])
    cv_s = din("cv_s", [c.PAST, FH * FD])
    clf_s = din("clf_s", [c.PAST, FH])
    mc_s = din("mc_s", [MH, MDK, MDV])
    mn_s = din("mn_s", [128, MH * 2])
    mm_s = din("mm_s", [1, MH])
    cst_f = din("cst_f", [128, 3 * 128])
    y_q = dout("y_q", [c.Q, D])
    y_s = dout("y_s", [c.DEC, D])
    o_fk = dout("o_fk", [S, 2 * FD])
    o_fv = dout("o_fv", [S, 2 * FD])
    o_flf = dout("o_flf", [S, 2])
    o_mc = dout("o_mc", [MDK, MDV])
    o_mn = dout("o_mn", [128, 2])
    o_mm = dout("o_mm", [1, 1])
    o_sfk = dout("o_sfk", [c.DEC, FW])
    o_sfv = dout("o_sfv", [c.DEC, FW])
    o_sflf = dout("o_sflf", [c.DEC, FH])
    o_smc = dout("o_smc", [MH, MDK, MDV])
    o_smn = dout("o_smn", [128, MH * 2])
    o_smm = dout("o_smm", [1, MH])
    NW = c.SPC * 128 * c.SLABW // S
    assert NW * S == c.SPC * 128 * c.SLABW
    cat = dscr("cat", [NW + 512, S])
    G = dscr("G", [8 * (NW + 512), S])
    W2l = cat[0:NW, :].rearrange("a b -> (a b)").rearrange("(r w) -> r w", w=c.SLABW)
    Gv = G.rearrange("(c m) s -> c m s", c=8)

    def W2v(idx):
        rk, j = idx // c.SPC, idx % c.SPC
        blk = Gv[rk, 0:NW, :].rearrange("a b -> (a b)").rearrange("(r w) -> r w", w=c.SLABW)
        return blk[j * 128:(j + 1) * 128, :]
    qT_d = dscr("qT_d", [2, 128, S])
    kT_d = dscr("kT_d", [2, 128, S])
    v_d = dscr("v_d", [2, 128, c.NB, 128])
    xbuf = cat[NW:NW + 512, :]
    hs_d = dscr("hs_d", [128, 16, c.DEC])
    Wsl = dscr("Wsl", [c.NSS * 128, c.KC * 512])

    SCALE = FD ** -0.5

    with ExitStack() as top:
        P = Prog(nc, top)
        A = P.add

        def mm(out, lhsT, rhs, st, sp, r, w):
            A("pe", lambda e: e.matmul(out, lhsT, rhs, start=st, stop=sp), r, w)

        def tp(out, in_, ident, r, w):
            A("pe", lambda e: e.transpose(out, in_, ident), r, w)

        def act(out, in_, func, r, w, bias=None, scale=None, accum=None):
            kw = {}
            if bias is not None:
                kw["bias"] = bias
            if scale is not None:
                kw["scale"] = scale
            if accum is not None:
                kw["accum_out"] = accum
            A("act", lambda e: e.activation(out=out, in_=in_, func=func, **kw), r, w)

        def ts(eng, out, in0, s1, s2, op0, op1, r, w):
            if s2 is None and eng == "pool":
                if op0 == ALU.add:
                    s2, op1 = 1.0, ALU.mult
                elif op0 == ALU.mult:
                    s2, op1 = 0.0, ALU.add
            if s2 is None:
                A(eng, lambda e: e.tensor_scalar(out=out, in0=in0, scalar1=s1, scalar2=None, op0=op0), r, w)
            else:
                A(eng, lambda e: e.tensor_scalar(out=out, in0=in0, scalar1=s1, scalar2=s2, op0=op0, op1=op1), r, w)

        def tt(eng, out, in0, in1, op, r, w):
            A(eng, lambda e: e.tensor_tensor(out=out, in0=in0, in1=in1, op=op), r, w)

        def stt(out, in0, s, in1, op0, op1, r, w):
            A("dve", lambda e: e.scalar_tensor_tensor(out=out, in0=in0, scalar=s, in1=in1, op0=op0, op1=op1), r, w)

        def recip(out, in_, r, w):
            A("dve", lambda e: e.reciprocal(out=out, in_=in_), r, w)

        def cp(eng, out, in_, r, w):
            if eng == "act":
                act(out, in_, AF.Copy, r, w)
            else:
                A(eng, lambda e: e.tensor_copy(out=out, in_=in_), r, w)

        def dma(out, in_, r, w, q="sp"):
            A(q, lambda e: e.dma_start(out=out, in_=in_), r, w, dma=True)

        sb = lambda st, name, shape, dtp: st.enter_context(nc.sbuf_tensor(name, list(shape), dtp))
        ps = lambda st, name, shape, dtp: st.enter_context(nc.psum_tensor(name, list(shape), dtp))

        cF = sb(top, "cF", [128, 384], F32)
        cB = sb(top, "cB", [128, 384], BF16)
        identF, triF, onesF = cF[:, 0:128], cF[:, 128:256], cF[:, 256:384]
        identB, triB, onesB = cB[:, 0:128], cB[:, 128:256], cB[:, 256:384]
        epsT = sb(top, "epsT", [128, 1], F32)
        Pk = sb(top, "Pk", [128, 2, c.NB], F32)
        Pend = sb(top, "Pend", [128, 2, c.NT1], F32)

        with ExitStack() as st, nc.Block() as block:
            dma(cF[:], cst_f, [], ["cF"])
            cp("dve", cB[:], cF[:], ["cF"], ["cB"])
            A("dve", lambda e: e.memset(epsT[:], EPS), [], ["eps"])
            P.barrier()

            w1fm = sb(st, "w1fm", [128, KC, 1024], BF16)
            w1tm = sb(st, "w1tm", [128, KC, 1284], BF16)
            w1v = w1.rearrange("(k p) n -> p k n", p=128)
            for k in range(KC):
                dma(w1fm[:, k, :], w1v[:, k, 0:1024], [], ["w1"], q="pool")
                dma(w1tm[:, k, :], w1v[:, k, 1024:2308], [], ["w1"], q="pool")

            for j in range(c.SPC):
                dma(W2l[j * 128:(j + 1) * 128, :], w2s[j * 128:(j + 1) * 128, :], [], [f"W2l{j}"], q="pool")

            for j in range(c.NSS):
                dma(Wsl[j * 128:(j + 1) * 128, :], wss[j * 128:(j + 1) * 128, :], [], [], q="pool")
            grep = sb(st, "grep", [128, D], F32)
            dma(grep[:], nmix.partition_broadcast(128), [], ["grep"])
            b4 = sb(st, "b4", [128, 4], F32)
            dma(b4[:], bias4.partition_broadcast(128), [], ["b4"])
            nh = sb(st, "nh", [128, 2], F32)
            dma(nh[:], normh_c, [], ["nh"])

            xa = [sb(st, f"xa{i}", [128, D], F32) for i in range(3)]
            junk = sb(st, "junk", [128, D], BF16)
            xs = [sb(st, f"xs{i}", [128, D], BF16) for i in range(2)]
            xT = [sb(st, f"xT{i}", [128, KC, 512], BF16) for i in range(2)]
            qT = sb(st, "qT", [128, 2, 512], BF16)
            kT = sb(st, "kT", [128, 2, 512], BF16)
            fst = [sb(st, f"fst{i}", [128, 4, 512], BF16) for i in range(2)]
            ktok = sb(st, "ktok", [128, 4, 256], BF16)
            vext = sb(st, "vext", [128, 4, 257], BF16)
            sigo = sb(st, "sigo", [128, 4, 256], F32)
            cst32 = [sb(st, f"cst32_{i}", [128, 512], F32) for i in range(2)]
            vbf = [sb(st, f"vbf{i}", [128, 2, 128], BF16) for i in range(2)]
            g4 = sb(st, "g4", [128, 4, 4], F32)
            e3 = sb(st, "e3", [128, 4, 3], F32)
            sp3 = sb(st, "sp3", [128, 4, 3], F32)
            lfo = [sb(st, f"lfo{i}", [128, 2], F32) for i in range(2)]
            carry = sb(st, "carry", [128, 2], F32)
            sm = sb(st, "sm", [128, 4, 16], F32)
            m0 = sb(st, "m0", [128, 1], F32)
            cS = sb(st, "cS", [128, 2, 256], F32)
            nS = sb(st, "nS", [128, 2], F32)
            cbf = sb(st, "cbf", [128, 2, 257], BF16)
            diag = sb(st, "diag", [128, 128], F32)
            stT = [sb(st, f"stT{i}", [128, 128], BF16) for i in range(2)]
            ku = [sb(st, f"ku{i}", [128, 256], BF16) for i in range(2)]
            hout = [sb(st, f"hout{i}", [128, 256], BF16) for i in range(2)]
            haT = [sb(st, f"haT{i}", [128, 2, 512], BF16) for i in range(2)]
            rs = sb(st, "rs", [128, 8], F32)

            psT = ps(st, "psT", [128, 8, 128], BF16)
            psF = [ps(st, f"psF{i}", [128, 512], F32) for i in range(2)]
            psA = ps(st, "psA", [128, 512], F32)
            psB = ps(st, "psB", [128, 512], F32)
            psC = ps(st, "psC", [128, 512], F32)
            psM = ps(st, "psM", [128, 512], F32)
            psK = ps(st, "psK", [128, 2, 256], F32)

            A("dve", lambda e: e.memset(carry[:], 0.0), [], ["carry"])
            A("dve", lambda e: e.memset(m0[:], 0.0), [], ["m0"])
            A("dve", lambda e: e.memset(cS[:], 0.0), [], ["cS"])
            A("dve", lambda e: e.memset(nS[:], 0.0), [], ["nS"])
            A("dve", lambda e: e.memset(cbf[:], 0.0), [], ["cbf"])
            for j in range(4):
                A("dve", lambda e, j=j: e.memset(vext[:, j, 256:257], 1.0), [], [f"vext{j}"])

            def norm_cast(xin, xout, gvec, rows, kx, kr, ky, kg):
                act(junk[:rows, :], xin, AF.Square, [kx], ["junk", kr], accum=rs[:rows, 0:1])
                act(rs[:rows, 1:2], rs[:rows, 0:1], AF.Ln, [kr], [kr], bias=epsT[:rows, :], scale=1.0 / D)
                act(rs[:rows, 2:3], rs[:rows, 1:2], AF.Exp, [kr], [kr], scale=-0.5)
                stt(xout, xin, rs[:rows, 2:3], gvec, ALU.mult, ALU.mult, [kx, kr, kg], [ky])

            SMW = 16

            def gates_and_chunk(ti, j, L, first_state_zero):
                kj = f"{j}"
                smj = sm[:, j, :]
                act(e3[:L, j, :], g4[:L, j, 1:4], AF.Exp, ["g4" + kj], ["e3" + kj], scale=-1.0)
                act(sp3[:L, j, :], e3[:L, j, :], AF.Ln, ["e3" + kj], ["sp3" + kj], bias=1.0)
                cum = psM[:L, 392:395]
                tot = psM[:, 396:399]
                mm(cum, triF[:L, :L], sp3[:L, j, :], True, True, ["sp3" + kj, "cF"], ["psM"])
                mm(tot, onesF[:L, :], sp3[:L, j, :], True, True, ["sp3" + kj, "cF"], ["psM"])
                blk = ti * 4 + j
                for h in range(2):
                    ts("dve", Pk[:L, h, blk:blk + 1], cum[:, 1 + h:2 + h], carry[:L, h:h + 1], None, ALU.add, None,
                       ["psM", "carry"], ["Pk"])
                tt("dve", carry[:, :], carry[:, :], tot[:, 1:3], ALU.add, ["psM", "carry"], ["carry"])
                lf = lfo[(ti * 4 + j) % 2]
                ts("dve", lf[:L, :], sp3[:L, j, 1:3], -1.0, None, ALU.mult, None, ["sp3" + kj], ["lfo%d" % ((ti * 4 + j) % 2)])
                dma(o_flf[blk * 128:blk * 128 + L, :], lf[:L, :], ["lfo%d" % ((ti * 4 + j) % 2)], [])
                tt("dve", smj[:L, 0:1], g4[:L, j, 0:1], cum[:, 0:1], ALU.add, ["g4" + kj, "psM"], ["sm" + kj])
                ts("dve", diag[:L, :L], identF[:L, :L], smj[:L, 0:1], None, ALU.mult, None, ["sm" + kj, "cF"], ["diag"])
                arow = psF[1][:, 0:L]
                mm(arow, onesF[:L, :], diag[:L, :L], True, True, ["diag", "cF"], ["psF1"])
                A("dve", lambda e: e.tensor_reduce(out=smj[:, 1:2], in_=arow, axis=AX.X, op=ALU.max), ["psF1"], ["sm" + kj])
                tt("dve", smj[:, 2:3], smj[:, 1:2], m0[:, :], ALU.max, ["sm" + kj, "m0"], ["sm" + kj])
                ts("dve", smj[:, 3:4], smj[:, 2:3], -1.0, None, ALU.mult, None, ["sm" + kj], ["sm" + kj])
                act(smj[:, 4:5], m0[:, :], AF.Exp, ["m0", "sm" + kj], ["sm" + kj], bias=smj[:, 3:4])
                act(smj[:L, 5:6], smj[:L, 0:1], AF.Exp, ["sm" + kj], ["sm" + kj], bias=smj[:L, 3:4])
                ts("dve", smj[:L, 5:6], smj[:L, 5:6], MDK ** -0.5, None, ALU.mult, None, ["sm" + kj], ["sm" + kj])
                act(smj[:L, 6:7], cum[:, 0:1], AF.Exp, ["psM", "sm" + kj], ["sm" + kj], bias=smj[:L, 3:4])
                tt("dve", m0[:, :], smj[:, 2:3], tot[:, 0:1], ALU.subtract, ["sm" + kj, "psM", "m0"], ["m0"])
                return smj

            def mlstm_chunk(j, L, smj, qTc, kTc, ktok_j, vext_j, sigo_j, houtb, hkey, pT_out):
                kj = f"{j}"
                g0, u, flo = smj[:, 4:5], smj[:L, 5:6], smj[:L, 6:7]
                ST = psA[:L, 0:L]
                for dc in range(2):
                    mm(ST, kTc[dc], qTc[dc], dc == 0, dc == 1, [f"qT{dc}", f"kT{dc}"], ["psA"])
                stb = stT[j % 2]
                stt(stb[:L, :L], ST, u, triB[:L, :L], ALU.mult, ALU.mult, ["psA", "sm" + kj, "cB"], ["stT%d" % (j % 2)])
                kub = ku[j % 2]
                ts("pool", kub[:L, :], ktok_j, u, None, ALU.mult, None, ["ktok" + kj, "sm" + kj], ["ku%d" % (j % 2)])
                act(cbf[:, :, 0:256], cS[:, :, :], AF.Copy, ["cS", "sm" + kj], ["cbf"], scale=g0)
                act(cbf[:, :, 256:257], nS[:, :].unsqueeze(2), AF.Copy, ["nS", "sm" + kj], ["cbf"], scale=g0)
                NE = psC[:L, 0:257]
                mm(NE, stb[:L, :L], vext_j, True, False, ["stT%d" % (j % 2), "vext" + kj], ["psC"])
                for dc in range(2):
                    mm(NE, qTc[dc], cbf[:, dc, :], False, dc == 1, [f"qT{dc}", "cbf"], ["psC"])
                for dc in range(2):
                    mm(psK[:, dc, :], kub[:L, dc * 128:(dc + 1) * 128], vext_j[:, 0:256], True, True,
                       ["ku%d" % (j % 2), "vext" + kj], ["psK"])
                    mm(psF[0][:, dc:dc + 1], kub[:L, dc * 128:(dc + 1) * 128], vext_j[:, 256:257], True, True,
                       ["ku%d" % (j % 2), "vext" + kj], ["psF0"])
                stt(cS[:, :, :], cS[:, :, :], g0, psK[:, :, :], ALU.mult, ALU.add, ["cS", "psK", "sm" + kj, "cbf"], ["cS"])
                stt(nS[:, :], nS[:, :], g0, psF[0][:, 0:2], ALU.mult, ALU.add, ["nS", "psF0", "sm" + kj, "cbf"], ["nS"])
                act(smj[:L, 14:15], NE[:, 256:257], AF.Abs, ["psC"], ["sm" + kj])
                tt("dve", smj[:L, 7:8], smj[:L, 14:15], flo, ALU.max, ["sm" + kj], ["sm" + kj])
                recip(smj[:L, 8:9], smj[:L, 7:8], ["sm" + kj], ["sm" + kj])
                act(junk[:L, 0:256], NE[:, 0:256], AF.Square, ["psC", "sm" + kj], ["junk", "sm" + kj],
                    scale=smj[:L, 8:9], accum=smj[:L, 9:10])
                act(smj[:L, 10:11], smj[:L, 9:10], AF.Ln, ["sm" + kj], ["sm" + kj], bias=epsT[:L, :], scale=1.0 / MDV)
                act(smj[:L, 11:12], smj[:L, 10:11], AF.Exp, ["sm" + kj], ["sm" + kj], scale=-0.5)
                tt("dve", smj[:L, 12:13], smj[:L, 11:12], smj[:L, 8:9], ALU.mult, ["sm" + kj], ["sm" + kj])
                stt(houtb[:L, :], NE[:, 0:256], smj[:L, 12:13], sigo_j, ALU.mult, ALU.mult,
                    ["psC", "sm" + kj, "sigo" + kj], [hkey])
                for dc in range(2):
                    tp(pT_out[dc], houtb[:L, dc * 128:(dc + 1) * 128], identB[:L, :L], [hkey, "cB"], ["psT"])

            xv = x_b.rearrange("(n p) d -> n p d", p=128)
            xbv = xbuf.rearrange("(k p) s -> p k s", p=128)
            NSUB = c.NT1 * 4

            def st_load(sub):
                dma(xa[sub % 3][:], xv[sub], [], [f"xa{sub % 3}"])

            def st_norm(sub):
                norm_cast(xa[sub % 3][:], xs[sub % 2][:], grep[:], 128, f"xa{sub % 3}", "rs", f"xs{sub % 2}", "grep")

            def st_tp(sub):
                ti, j = sub // 4, sub % 4
                xTt, kxT, xsb = xT[ti % 2], f"xT{ti % 2}", xs[sub % 2]
                for k0 in range(0, KC, 4):
                    nk = min(4, KC - k0)
                    for k in range(nk):
                        tp(psT[:, k, :], xsb[:, (k0 + k) * 128:(k0 + k + 1) * 128], identB, [f"xs{sub % 2}", "cB"], ["psT"])
                    cp("act" if (k0 // 4) % 2 == 0 else "dve", xTt[:, k0:k0 + nk, j * 128:(j + 1) * 128], psT[:, 0:nk, :],
                       ["psT"], [kxT + f".{j}.{k0 // 4}"])

            def st_mm(sub):
                ti, j = sub // 4, sub % 4
                xTt, kxT = xT[ti % 2], f"xT{ti % 2}"
                for k in range(KC):
                    l = xTt[:, k, j * 128:(j + 1) * 128]
                    kk = [kxT + f".{j}.{k // 4}", "w1"]
                    mm(psA[:, :], l, w1tm[:, k, 0:512], k == 0, k == KC - 1, kk, ["psA"])
                    mm(psB[:, 0:260], l, w1tm[:, k, 512:772], k == 0, k == KC - 1, kk, ["psB"])
                    mm(psC[:, :], l, w1tm[:, k, 772:1284], k == 0, k == KC - 1, kk, ["psC"])

            def st_evac(sub):
                ti, j = sub // 4, sub % 4
                kj = f"{j}"
                cp("act", ktok[:, j, :], psA[:, 0:256], ["psA"], ["ktok" + kj])
                cp("act", vext[:, j, 0:256], psA[:, 256:512], ["psA"], ["vext" + kj])
                act(sigo[:, j, :], psB[:, 0:256], AF.Exp, ["psB"], ["sigo" + kj], scale=-1.0)
                ts("pool", sigo[:, j, :], sigo[:, j, :], 1.0, None, ALU.add, None, ["sigo" + kj], ["sigo" + kj])
                recip(sigo[:, j, :], sigo[:, j, :], ["sigo" + kj], ["sigo" + kj])
                tt("dve", g4[:, j, :], psB[:, 256:260], b4[:, :], ALU.add, ["psB", "b4"], ["g4" + kj])
                c32 = cst32[sub % 2]
                cp("dve", c32[:, :], psC[:, :], ["psC"], [f"c32_{sub % 2}"])
                c3v = c32[:, :].rearrange("p (h t d) -> p h t d", h=2, t=2)
                dma(o_fk[sub * 128:(sub + 1) * 128, :].rearrange("p (h d) -> p h d", h=2), c3v[:, :, 0, :],
                    [f"c32_{sub % 2}"], [])
                dma(o_fv[sub * 128:(sub + 1) * 128, :].rearrange("p (h d) -> p h d", h=2), c3v[:, :, 1, :],
                    [f"c32_{sub % 2}"], [])
                vb = vbf[sub % 2]
                cp("pool", vb[:, :, :], c3v[:, :, 1, :], [f"c32_{sub % 2}"], [f"vbf{sub % 2}"])
                dma(v_d[:, :, sub, :].rearrange("h p d -> p h d"), vb[:, :, :], [f"vbf{sub % 2}"], [])

            st_load(0)
            if NSUB > 1:
                st_load(1)
            st_norm(0)
            st_tp(0)
            for ti in range(c.NT1):
                xTt = xT[ti % 2]
                kxT = f"xT{ti % 2}"
                for j in range(4):
                    sub = ti * 4 + j
                    if sub + 2 < NSUB:
                        st_load(sub + 2)
                    if sub + 1 < NSUB and j < 3:
                        st_norm(sub + 1)
                    st_mm(sub)
                    if sub + 1 < NSUB and j < 3:
                        st_tp(sub + 1)
                    st_evac(sub)
                fs_ = fst[ti % 2]
                for cc in range(8):
                    pf = psF[cc % 2]
                    for k in range(KC):
                        mm(pf[:, :], w1fm[:, k, cc * 128:(cc + 1) * 128], xTt[:, k, :], k == 0, k == KC - 1,
                           [kxT + f".{j}.{k // 4}" for j in range(4)] + ["w1"], [f"psF{cc % 2}"])
                    eng = "act" if cc % 2 == 0 else "dve"
                    if cc < 2:
                        cp(eng, qT[:, cc, :], pf[:, :], [f"psF{cc % 2}"], [f"qT{cc}"])
                    elif cc < 4:
                        cp(eng, kT[:, cc - 2, :], pf[:, :], [f"psF{cc % 2}"], [f"kT{cc - 2}"])
                    else:
                        cp(eng, fs_[:, cc - 4, :], pf[:, :], [f"psF{cc % 2}"], [f"fst{ti % 2}.{cc - 4}"])
                dma(qT_d[:, :, ti * 512:(ti + 1) * 512].rearrange("h p s -> p h s"), fs_[:, 0:2, :],
                    [f"fst{ti % 2}.0", f"fst{ti % 2}.1"], [])
                dma(kT_d[:, :, ti * 512:(ti + 1) * 512].rearrange("h p s -> p h s"), fs_[:, 2:4, :],
                    [f"fst{ti % 2}.2", f"fst{ti % 2}.3"], [])
                hb_ = haT[ti % 2]
                for j in range(4):
                    smj = gates_and_chunk(ti, j, 128, False)
                    qTc = [qT[:, dc, j * 128:(j + 1) * 128] for dc in range(2)]
                    kTc = [kT[:, dc, j * 128:(j + 1) * 128] for dc in range(2)]
                    hk = f"hout{j % 2}"
                    mlstm_chunk(j, 128, smj, qTc, kTc, ktok[:, j, :], vext[:, j, :], sigo[:, j, :], hout[j % 2], hk,
                                [psT[:, 4 + dc, :] for dc in range(2)])
                    for dc in range(2):
                        ts("dve", hb_[:, dc, j * 128:(j + 1) * 128], psT[:, 4 + dc, :], nh[:, dc:dc + 1], None, ALU.mult, None,
                           ["psT", "nh"], [f"haT{ti % 2}"])
                    if j == 3:
                        ts("dve", Pend[:, :, ti:ti + 1], carry[:, :].unsqueeze(2), 1.0, None, ALU.mult, None, ["carry"], ["Pend"])
                dma(xbv[:, 0:2, ti * 512:(ti + 1) * 512], hb_[:, :, :], [f"haT{ti % 2}"], [])
                if ti + 1 < c.NT1:
                    st_norm((ti + 1) * 4)
                    st_tp((ti + 1) * 4)
            dma(o_mc.rearrange("(k p) e -> p k e", p=128), cS[:, :, :], ["cS"], [])
            dma(o_mn, nS[:, :], ["nS"], [])
            dma(o_mm, m0[0:1, :], ["m0"], [])
            P.barrier()
            P.emit(block)
        if float(os.environ.get('KSTOP', '9')) <= 1:
            return nc

        with ExitStack() as st, nc.Block() as block:
            KTc = sb(st, "KTc", [128, 2, S], BF16)
            Vc = sb(st, "Vc", [128, 2, c.NB, 128], BF16)
            qtl = [sb(st, f"qtl{i}", [128, 2, 512], BF16) for i in range(2)]
            pt = [sb(st, f"pt{i}", [128, 512], BF16) for i in range(4)]
            bias_t = [sb(st, f"biast{i}", [128, 2, c.NB], F32) for i in range(2)]
            rD = sb(st, "rD", [128, 512], F32)
            hbT = [sb(st, f"hbT{i}", [128, 512], BF16) for i in range(2)]
            psS = [ps(st, f"psS{i}", [128, 512], F32) for i in range(3)]
            psO = [ps(st, f"psO{i}", [128, 512], F32) for i in range(2)]
            psDn = [ps(st, f"psDn{i}", [128, 512], F32) for i in range(2)]
            NCH = 8 if c.NB >= 8 else c.NB
            bpc = c.NB // NCH
            for q_ in range(NCH):
                s0, s1 = q_ * bpc * 128, (q_ + 1) * bpc * 128
                dma(KTc[:, :, s0:s1], kT_d[:, :, s0:s1].rearrange("h p s -> p h s"), [], [f"KTc{q_}"])
                dma(Vc[:, :, q_ * bpc:(q_ + 1) * bpc, :], v_d[:, :, q_ * bpc:(q_ + 1) * bpc, :].rearrange("h p b d -> p h b d"),
                    ["v_d"], [f"Vc{q_}"])
            xbv = xbuf.rearrange("(k p) s -> p k s", p=128)
            items = []
            for ti in range(c.NT1):
                for h in range(2):
                    for kb in range(4 * ti + 4):
                        items.append((ti, h, kb))
            LA = 2

            def stage_a(i):
                ti, h, kb = items[i]
                ql = qtl[ti % 2]
                bt = bias_t[ti % 2]
                nkb = 4 * ti + 4
                if h == 0 and kb == 0:
                    dma(ql[:, :, :], qT_d[:, :, ti * 512:(ti + 1) * 512].rearrange("h p s -> p h s"), [], [f"qtl{ti % 2}"])
                    for hh in range(2):
                        ts("dve", bt[:, hh, 0:nkb], Pk[:, hh, 0:nkb], Pend[:, hh, ti:ti + 1], None, ALU.subtract, None,
                           ["Pk", "Pend"], [f"biast{ti % 2}"])
                n0 = max(0, kb - 4 * ti) * 128
                pS = psS[i % 3]
                pb = pt[i % 4]
                kq = kb // bpc
                mm(pS[:, n0:512], KTc[:, h, kb * 128:(kb + 1) * 128], ql[:, h, n0:512], True, True,
                   [f"KTc{kq}", f"qtl{ti % 2}"], [f"psS{i % 3}"])
                act(pb[:, n0:512], pS[:, n0:512], AF.Exp, [f"psS{i % 3}", f"biast{ti % 2}"], [f"pt{i % 4}"],
                    bias=bt[:, h, kb:kb + 1], scale=SCALE)
                if kb >= 4 * ti:
                    tt("pool", pb[:, n0:n0 + 128], pb[:, n0:n0 + 128], triB, ALU.mult, [f"pt{i % 4}", "cB"], [f"pt{i % 4}"])

            def stage_b(i):
                ti, h, kb = items[i]
                nkb = 4 * ti + 4
                n0 = max(0, kb - 4 * ti) * 128
                pO, pD = psO[h], psDn[h]
                pb = pt[i % 4]
                kq = kb // bpc
                mm(pO[:, n0:512], Vc[:, h, kb, :], pb[:, n0:512], kb == 0, kb == nkb - 1,
                   [f"Vc{kq}", f"pt{i % 4}"], [f"psO{h}"])
                mm(pD[:, n0:512], onesB, pb[:, n0:512], kb == 0, kb == nkb - 1, ["cB", f"pt{i % 4}"], [f"psDn{h}"])
                if kb == nkb - 1:
                    recip(rD[:, :], pD[:, :], [f"psDn{h}"], ["rD"])
                    ob = hbT[(ti * 2 + h) % 2]
                    tt("dve", ob[:, :], pO[:, :], rD[:, :], ALU.mult, [f"psO{h}", "rD"], [f"hbT{(ti * 2 + h) % 2}"])
                    dma(xbv[:, 2 + h, ti * 512:(ti + 1) * 512], ob[:, :], [f"hbT{(ti * 2 + h) % 2}"], [f"xbuf{ti}.{h}"])

            for i in range(min(LA, len(items))):
                stage_a(i)
            for i in range(len(items)):
                if i + LA < len(items):
                    stage_a(i + LA)
                stage_b(i)
            A("pool", lambda e: e.collective_compute("AllGather", ALU.bypass, replica_groups=[list(range(8))],
                                                     ins=[cat], outs=[G]),
              [f"xbuf{ti}.{h}" for ti in range(c.NT1) for h in range(2)], [], cc=True)
            P.barrier()
            P.emit(block)
        if float(os.environ.get('KSTOP', '9')) <= 2:
            return nc

        NQ = c.DEC
        with ExitStack() as st, nc.Block() as block:
            grep = sb(st, "grepS", [128, D], F32)
            dma(grep[:NQ, :], nmix.partition_broadcast(NQ), [], ["grep"])
            xa_s = sb(st, "xa_s", [128, D], F32)
            junk = sb(st, "junkS", [128, D], BF16)
            xs_s = sb(st, "xs_s", [128, D], BF16)
            xT_s = sb(st, "xT_s", [128, KC, NQ], BF16)
            rs = sb(st, "rsS", [128, 8], F32)
            slab = [sb(st, f"slabS{i}", [128, KC, 512], BF16) for i in range(3)]
            mqT = sb(st, "mqT", [128, 8, NQ], BF16)
            mkT = sb(st, "mkT", [128, 8, NQ], BF16)
            fqT = sb(st, "fqT", [128, 8, NQ], BF16)
            fkT = sb(st, "fkT", [128, 8, NQ], BF16)
            ktok = sb(st, "ktokS", [128, 4, 256], BF16)
            vext = sb(st, "vextS", [128, 4, 257], BF16)
            sigo = sb(st, "sigoS", [128, 4, 256], F32)
            frow = [sb(st, f"frow{i}", [128, FW], F32) for i in range(2)]
            vnb = sb(st, "vnb", [128, FW], BF16)
            g16 = sb(st, "g16", [128, 16], F32)
            b16 = sb(st, "b16", [128, 16], F32)
            dma(b16[:NQ, :], bias_s.partition_broadcast(NQ), [], ["b16"])
            nh8 = sb(st, "nh8", [128, 8], F32)
            dma(nh8[:], normh_s, [], ["nh8"])
            e12 = sb(st, "e12", [128, 12], F32)
            sp12 = sb(st, "sp12", [128, 12], F32)
            sm = sb(st, "smS", [128, 4, 16], F32)
            m0a = sb(st, "m0a", [128, 4], F32)
            cS4 = sb(st, "cS4", [128, 4, 2, 256], F32)
            nS4 = sb(st, "nS4", [128, 8], F32)
            cbf = sb(st, "cbfS", [128, 2, 257], BF16)
            diag = sb(st, "diagS", [128, 128], F32)
            stT = [sb(st, f"stTS{i}", [128, 128], BF16) for i in range(2)]
            ku = [sb(st, f"kuS{i}", [128, 256], BF16) for i in range(2)]
            hout = [sb(st, f"houtS{i}", [128, 256], BF16) for i in range(2)]
            hsT = sb(st, "hsT", [128, 16, NQ], BF16)
            kb_p = sb(st, "kb_p", [128, 2, FW], BF16)
            Vp = sb(st, "Vp", [128, c.PB, FW], BF16)
            KTp = sb(st, "KTp", [128, FH, c.PAST], BF16)
            lfp = sb(st, "lfp", [128, c.PB, FH], F32)
            Pkp = sb(st, "Pkp", [128, FH, c.PB + 1], F32)
            carr8 = sb(st, "carr8", [128, FH], F32)
            sc_s = sb(st, "sc_s", [128, c.PB, NQ], F32)
            pts = sb(st, "pts", [128, c.PB + 1, NQ], BF16)
            rDs = sb(st, "rDs", [128, FH, NQ], F32)

            psT = ps(st, "psTS", [128, 8, 128], BF16)
            psR = [ps(st, f"psR{i}", [128, 512], F32) for i in range(2)]
            psF = [ps(st, f"psFS{i}", [128, 512], F32) for i in range(2)]
            psM = ps(st, "psMS", [128, 512], F32)
            psK = ps(st, "psKS", [128, 2, 256], F32)
            psX = ps(st, "psXS", [128, 512], F32)

            dma(xa_s[:NQ, :], x_s, [], ["xa"])
            act(junk[:NQ, :], xa_s[:NQ, :], AF.Square, ["xa"], ["junk", "rs"], accum=rs[:NQ, 0:1])
            act(rs[:NQ, 1:2], rs[:NQ, 0:1], AF.Ln, ["rs"], ["rs"], bias=epsT[:NQ, :], scale=1.0 / D)
            act(rs[:NQ, 2:3], rs[:NQ, 1:2], AF.Exp, ["rs"], ["rs"], scale=-0.5)
            stt(xs_s[:NQ, :], xa_s[:NQ, :], rs[:NQ, 2:3], grep[:NQ, :], ALU.mult, ALU.mult, ["xa", "rs", "grep"], ["xs"])
            for k0 in range(0, KC, 8):
                nk = min(8, KC - k0)
                for k in range(nk):
                    tp(psT[:, k, 0:NQ], xs_s[:NQ, (k0 + k) * 128:(k0 + k + 1) * 128], identB[:NQ, :NQ], ["xs", "cB"], ["psT"])
                cp("dve", xT_s[:, k0:k0 + nk, :], psT[:, 0:nk, 0:NQ], ["psT"], ["xT"])
            ckv = ck_s.rearrange("(b p) f -> p b f", p=128)
            cvv = cv_s.rearrange("(b p) f -> p b f", p=128)
            for b in range(c.PB):
                dma(Vp[:, b, :], cvv[:, b, :], [], [f"Vp{b}"], q="pool")
            dma(lfp[:, :, :], clf_s.rearrange("(b p) h -> p b h", p=128), [], ["lfp"])
            dma(cS4[:, :, :, :], mc_s.rearrange("h (k p) e -> p h k e", p=128), [], ["cS4"])
            dma(nS4[:, :], mn_s, [], ["nS4"])
            dma(m0a[:, :], mm_s.partition_broadcast(128), [], ["m0a"])
            for b in range(c.PB):
                dma(kb_p[:, b % 2, :], ckv[:, b, :], [], [f"kbp{b % 2}"], q="pool")
                for h in range(FH):
                    tp(psT[:, h, :], kb_p[:, b % 2, h * 128:(h + 1) * 128], identB, [f"kbp{b % 2}", "cB"], ["psT"])
                cp("act" if b % 2 == 0 else "dve", KTp[:, :, b * 128:(b + 1) * 128], psT[:, :, :], ["psT"], ["KTp"])
            ri = 0
            fi = 0
            for si in range(c.NSS):
                sl = slab[si % 3]
                ks = f"slabS{si % 3}"
                dma(sl[:, :, :], Wsl[si * 128:(si + 1) * 128, :].rearrange("p (k n) -> p k n", k=KC), [], [ks])
                grp = si // 2
                do_fm = grp in (0, 1, 4, 5)
                do_tm = grp in (1, 2, 3, 5, 6, 7)
                ncol = 16 if si == 14 else 512
                if do_tm:
                    pr = psR[ri % 2]
                    kr = f"psR{ri % 2}"
                    ri += 1
                    for k in range(KC):
                        mm(pr[:NQ, 0:ncol], xT_s[:, k, :], sl[:, k, 0:ncol], k == 0, k == KC - 1, ["xT", ks], [kr])
                    half = si % 2
                    if grp == 1:
                        for hh in range(2):
                            cp("act", ktok[:NQ, half * 2 + hh, :], pr[:NQ, hh * 256:(hh + 1) * 256], [kr], [f"ktok{half * 2 + hh}"])
                    elif grp == 2:
                        for hh in range(2):
                            cp("act", vext[:NQ, half * 2 + hh, 0:256], pr[:NQ, hh * 256:(hh + 1) * 256], [kr], [f"vext{half * 2 + hh}"])
                    elif grp == 3:
                        for hh in range(2):
                            hd = half * 2 + hh
                            act(sigo[:NQ, hd, :], pr[:NQ, hh * 256:(hh + 1) * 256], AF.Exp, [kr], [f"sigo{hd}"], scale=-1.0)
                            ts("pool", sigo[:NQ, hd, :], sigo[:NQ, hd, :], 1.0, None, ALU.add, None, [f"sigo{hd}"], [f"sigo{hd}"])
                            recip(sigo[:NQ, hd, :], sigo[:NQ, hd, :], [f"sigo{hd}"], [f"sigo{hd}"])
                    elif grp == 5:
                        cp("dve", frow[0][:NQ, half * 512:(half + 1) * 512], pr[:NQ, :], [kr], ["frow0"])
                    elif grp == 6:
                        cp("dve", frow[1][:NQ, half * 512:(half + 1) * 512], pr[:NQ, :], [kr], ["frow1"])
                    else:
                        tt("dve", g16[:NQ, :], pr[:NQ, 0:16], b16[:NQ, :], ALU.add, [kr, "b16"], ["g16"])
                if do_fm:
                    pf = psF[fi % 2]
                    kf = f"psFS{fi % 2}"
                    fi += 1
                    pfv = pf[:, 0:4 * NQ].rearrange("p (c n) -> p c n", c=4)
                    for cc in range(4):
                        for k in range(KC):
                            mm(pfv[:, cc, :], sl[:, k, cc * 128:(cc + 1) * 128], xT_s[:, k, :], k == 0, k == KC - 1, ["xT", ks], [kf])
                    dst = {0: mqT, 1: mkT, 4: fqT, 5: fkT}[grp]
                    half = si % 2
                    cp("act", dst[:, half * 4:(half + 1) * 4, :], pfv, [kf], [{0: "mqT", 1: "mkT", 4: "fqT", 5: "fkT"}[grp]])
            dma(o_sfk, frow[0][:NQ, :], ["frow0"], [])
            dma(o_sfv, frow[1][:NQ, :], ["frow1"], [])
            cp("pool", vnb[:NQ, :], frow[1][:NQ, :], ["frow1"], ["vnb"])
            for j in range(4):
                A("dve", lambda e, j=j: e.memset(vext[:, j, 256:257], 1.0), [], [f"vext{j}"])
            act(e12[:NQ, :], g16[:NQ, 4:16], AF.Exp, ["g16"], ["e12"], scale=-1.0)
            act(sp12[:NQ, :], e12[:NQ, :], AF.Ln, ["e12"], ["sp12"], bias=1.0)
            lfs = sb(st, "lfs", [128, FH], F32)
            ts("dve", lfs[:NQ, :], sp12[:NQ, 4:12], -1.0, None, ALU.mult, None, ["sp12"], ["lfs"])
            dma(o_sflf, lfs[:NQ, :], ["lfs"], [])
            cumN = psM[:NQ, 392:404]
            totN = psM[:, 404:416]
            mm(cumN, triF[:NQ, :NQ], sp12[:NQ, :], True, True, ["sp12", "cF"], ["psM"])
            mm(totN, onesF[:NQ, :], sp12[:NQ, :], True, True, ["sp12", "cF"], ["psM"])
            cum12 = sb(st, "cum12", [128, 12], F32)
            tot12 = sb(st, "tot12", [128, 12], F32)
            cp("dve", cum12[:NQ, :], cumN, ["psM"], ["cum12"])
            cp("dve", tot12[:, :], totN, ["psM"], ["tot12"])
            for h in range(MH):
                kj = f"{h}"
                smj = sm[:, h, :]
                L = NQ
                tt("dve", smj[:L, 0:1], g16[:L, h:h + 1], cum12[:L, h:h + 1], ALU.add, ["g16", "cum12"], ["sm" + kj])
                ts("dve", diag[:L, :L], identF[:L, :L], smj[:L, 0:1], None, ALU.mult, None, ["sm" + kj, "cF"], ["diag"])
                arow = psF[0][:, 0:L]
                mm(arow, onesF[:L, :], diag[:L, :L], True, True, ["diag", "cF"], ["psFS0"])
                A("dve", lambda e, smj=smj, arow=arow: e.tensor_reduce(out=smj[:, 1:2], in_=arow, axis=AX.X, op=ALU.max),
                  ["psFS0"], ["sm" + kj])
                tt("dve", smj[:, 2:3], smj[:, 1:2], m0a[:, h:h + 1], ALU.max, ["sm" + kj, "m0a"], ["sm" + kj])
                ts("dve", smj[:, 3:4], smj[:, 2:3], -1.0, None, ALU.mult, None, ["sm" + kj], ["sm" + kj])
                act(smj[:, 4:5], m0a[:, h:h + 1], AF.Exp, ["m0a", "sm" + kj], ["sm" + kj], bias=smj[:, 3:4])
                act(smj[:L, 5:6], smj[:L, 0:1], AF.Exp, ["sm" + kj], ["sm" + kj], bias=smj[:L, 3:4])
                ts("dve", smj[:L, 5:6], smj[:L, 5:6], MDK ** -0.5, None, ALU.mult, None, ["sm" + kj], ["sm" + kj])
                act(smj[:L, 6:7], cum12[:L, h:h + 1], AF.Exp, ["cum12", "sm" + kj], ["sm" + kj], bias=smj[:L, 3:4])
                mnew = sm[:, h, 13:14]
                tt("dve", mnew, smj[:, 2:3], tot12[:, h:h + 1], ALU.subtract, ["sm" + kj, "tot12"], ["sm" + kj])
                g0, u, flo = smj[:, 4:5], smj[:L, 5:6], smj[:L, 6:7]
                ST = psR[0][:L, 0:L]
                for dc in range(2):
                    mm(ST, mkT[:, h * 2 + dc, :], mqT[:, h * 2 + dc, :], dc == 0, dc == 1, ["mqT", "mkT"], ["psR0"])
                stb = stT[h % 2]
                stt(stb[:L, :L], ST, u, triB[:L, :L], ALU.mult, ALU.mult, ["psR0", "sm" + kj, "cB"], [f"stT{h % 2}"])
                kub = ku[h % 2]
                ts("pool", kub[:L, :], ktok[:L, h, :], u, None, ALU.mult, None, ["ktok" + kj, "sm" + kj], [f"ku{h % 2}"])
                act(cbf[:, :, 0:256], cS4[:, h, :, :], AF.Copy, ["cS4", "sm" + kj], ["cbf"], scale=g0)
                act(cbf[:, :, 256:257], nS4[:, h * 2:h * 2 + 2].unsqueeze(2), AF.Copy, ["nS4", "sm" + kj], ["cbf"], scale=g0)
                NE = psR[1][:L, 0:257]
                mm(NE, stb[:L, :L], vext[:L, h, :], True, False, [f"stT{h % 2}", "vext" + kj], ["psR1"])
                for dc in range(2):
                    mm(NE, mqT[:, h * 2 + dc, :], cbf[:, dc, :], False, dc == 1, ["mqT", "cbf"], ["psR1"])
                for dc in range(2):
                    mm(psK[:, dc, :], kub[:L, dc * 128:(dc + 1) * 128], vext[:L, h, 0:256], True, True,
                       [f"ku{h % 2}", "vext" + kj], ["psK"])
                    mm(psF[1][:, dc:dc + 1], kub[:L, dc * 128:(dc + 1) * 128], vext[:L, h, 256:257], True, True,
                       [f"ku{h % 2}", "vext" + kj], ["psFS1"])
                stt(cS4[:, h, :, :], cS4[:, h, :, :], g0, psK[:, :, :], ALU.mult, ALU.add, ["cS4", "psK", "sm" + kj, "cbf"], ["cS4"])
                stt(nS4[:, h * 2:h * 2 + 2], nS4[:, h * 2:h * 2 + 2], g0, psF[1][:, 0:2], ALU.mult, ALU.add,
                    ["nS4", "psFS1", "sm" + kj, "cbf"], ["nS4"])
                act(smj[:L, 14:15], NE[:, 256:257], AF.Abs, ["psR1"], ["sm" + kj])
                tt("dve", smj[:L, 7:8], smj[:L, 14:15], flo, ALU.max, ["sm" + kj], ["sm" + kj])
                recip(smj[:L, 8:9], smj[:L, 7:8], ["sm" + kj], ["sm" + kj])
                act(junk[:L, 0:256], NE[:, 0:256], AF.Square, ["psR1", "sm" + kj], ["junk", "sm" + kj],
                    scale=smj[:L, 8:9], accum=smj[:L, 9:10])
                act(smj[:L, 10:11], smj[:L, 9:10], AF.Ln, ["sm" + kj], ["sm" + kj], bias=epsT[:L, :], scale=1.0 / MDV)
                act(smj[:L, 11:12], smj[:L, 10:11], AF.Exp, ["sm" + kj], ["sm" + kj], scale=-0.5)
                tt("dve", smj[:L, 12:13], smj[:L, 11:12], smj[:L, 8:9], ALU.mult, ["sm" + kj], ["sm" + kj])
                hb_ = hout[h % 2]
                stt(hb_[:L, :], NE[:, 0:256], smj[:L, 12:13], sigo[:L, h, :], ALU.mult, ALU.mult,
                    ["psR1", "sm" + kj, "sigo" + kj], [f"hout{h % 2}"])
                for dc in range(2):
                    tp(psT[:, dc, 0:L], hb_[:L, dc * 128:(dc + 1) * 128], identB[:L, :L], [f"hout{h % 2}", "cB"], ["psT"])
                    ts("dve", hsT[:, h * 2 + dc, :], psT[:, dc, 0:L], nh8[:, h * 2 + dc:h * 2 + dc + 1], None, ALU.mult, None,
                       ["psT", "nh8"], ["hsT"])
            msout = sb(st, "msout", [128, MH], F32)
            for h in range(MH):
                cp("dve", msout[:, h:h + 1], sm[:, h, 13:14], [f"sm{h}"], ["msout"])
            dma(o_smc.rearrange("h (k p) e -> p h k e", p=128), cS4[:, :, :, :], ["cS4"], [])
            dma(o_smn, nS4[:, :], ["nS4"], [])
            dma(o_smm, msout[0:1, :], ["msout"], [])
            A("dve", lambda e: e.memset(carr8[:], 0.0), [], ["carr8"])
            A("dve", lambda e: e.memset(Pkp[:], 0.0), [], ["Pkp"])
            sp_p = sb(st, "sp_p", [128, c.PB, FH], F32)
            ts("dve", sp_p[:, :, :], lfp[:, :, :], -1.0, None, ALU.mult, None, ["lfp"], ["sp_p"])
            cumP = psR[0][:, 0:c.PB * FH]
            totP = psR[1][:, 0:c.PB * FH]
            spf = sp_p[:, :, :].rearrange("p b h -> p (b h)")
            mm(cumP, triF, spf, True, True, ["sp_p", "cF"], ["psR0"])
            mm(totP, onesF, spf, True, True, ["sp_p", "cF"], ["psR1"])
            for b in range(c.PB):
                tt("dve", Pkp[:, :, b], cumP[:, b * FH:(b + 1) * FH], carr8[:, :], ALU.add, ["psR0", "carr8"], ["Pkp"])
                tt("dve", carr8[:, :], carr8[:, :], totP[:, b * FH:(b + 1) * FH], ALU.add, ["psR1", "carr8"], ["carr8"])
            tt("dve", Pkp[:NQ, :, c.PB], cum12[:NQ, 4:12], carr8[:NQ, :], ALU.add, ["cum12", "carr8"], ["Pkp"])
            tt("dve", carr8[:, :], carr8[:, :], tot12[:, 4:12], ALU.add, ["tot12", "carr8"], ["carr8"])
            bias_k = sb(st, "bias_k", [128, FH, c.PB + 1], F32)
            tt("dve", bias_k[:, :, :], Pkp[:, :, :], carr8[:, :].unsqueeze(2).to_broadcast([128, FH, c.PB + 1]), ALU.subtract,
               ["Pkp", "carr8"], ["bias_k"])
            Oall = psX[:, 0:FH * NQ].rearrange("p (h n) -> p h n", h=FH)
            Dall = psK[:, :, :].rearrange("p a b -> p (a b)")[:, 0:FH * NQ].rearrange("p (h n) -> p h n", h=FH)
            for h in range(FH):
                pS = psR[h % 2]
                kS = f"psR{h % 2}"
                pSv = pS[:, 0:c.PB * NQ].rearrange("p (b n) -> p b n", b=c.PB)
                for b in range(c.PB):
                    mm(pSv[:, b, :], KTp[:, h, b * 128:(b + 1) * 128], fqT[:, h, :], True, True, ["KTp", "fqT"], [kS])
                stt(sc_s[:, :, :], pSv, SCALE, bias_k[:, h, 0:c.PB].unsqueeze(2).to_broadcast([128, c.PB, NQ]), ALU.mult, ALU.add,
                    [kS, "bias_k"], ["sc_s"])
                act(pts[:, 0:c.PB, :], sc_s[:, :, :], AF.Exp, ["sc_s"], ["pts"])
                pN = psM[:NQ, 0:NQ]
                mm(pN, fkT[:, h, :], fqT[:, h, :], True, True, ["fkT", "fqT"], ["psM"])
                act(pts[:NQ, c.PB, :], pN, AF.Exp, ["psM", "bias_k"], ["pts"], bias=bias_k[:NQ, h, c.PB:c.PB + 1], scale=SCALE)
                tt("pool", pts[:NQ, c.PB, :], pts[:NQ, c.PB, :], triB[:NQ, :NQ], ALU.mult, ["pts", "cB"], ["pts"])
                for b in range(c.PB):
                    mm(Oall[:, h, :], Vp[:, b, h * 128:(h + 1) * 128], pts[:, b, :], b == 0, False, [f"Vp{b}", "pts"], ["psX"])
                    mm(Dall[:, h, :], onesB, pts[:, b, :], b == 0, False, ["cB", "pts"], ["psK"])
                mm(Oall[:, h, :], vnb[:NQ, h * 128:(h + 1) * 128], pts[:NQ, c.PB, :], False, True, ["vnb", "pts"], ["psX"])
                mm(Dall[:, h, :], onesB[:NQ, :], pts[:NQ, c.PB, :], False, True, ["cB", "pts"], ["psK"])
            recip(rDs[:, :, :], Dall, ["psK"], ["rDs"])
            tt("dve", hsT[:, 8:16, :], Oall, rDs[:, :, :], ALU.mult, ["psX", "rDs"], ["hsT"])
            dma(hs_d, hsT[:, :, :], ["hsT"], [])
            P.barrier()
            P.emit(block)
        if float(os.environ.get('KSTOP', '9')) <= 3:
            return nc

        with ExitStack() as st, nc.Block() as block:
            xr = sb(st, "xr", [128, 4, D], F32)
            gv = sb(st, "gv", [128, D], F32)
            junk = sb(st, "junk4", [128, D], BF16)
            xs = [sb(st, f"xs4_{i}", [128, D], BF16) for i in range(2)]
            xT = sb(st, "xT4", [128, KC, 512], BF16)
            big = sb(st, "big", [128, max(c.FCH, 16 + KC), 512], BF16)
            slab = [sb(st, f"slab{i}", [128, c.SLABW], BF16) for i in range(3)]
            tmp = [sb(st, f"tmp{i}", [128, 512], F32) for i in range(5)]
            rs = sb(st, "rs4", [128, 8], F32)
            psT = ps(st, "psT4", [128, 8, 128], BF16)
            psX = [ps(st, f"psX{i}", [128, 512], F32) for i in range(4)]
            psY = [ps(st, f"psY{i}", [128, 512], F32) for i in range(2)]
            hT = lambda k: big[:, k, :]
            mT = lambda m: big[:, 16 + m, :]

            xg5 = Gv[:, NW:NW + 512, :].rearrange("(g r) (k p) s -> g p r k s", g=2, k=4)
            ctr = {"sl": 0}

            def load_slab(idx, width):
                i = ctr["sl"] % 3
                ctr["sl"] += 1
                dma(slab[i][:, 0:width], W2v(idx)[:, 0:width], [], [f"slab{i}"])
                return slab[i], f"slab{i}"

            def norm_t(NT, gsrc, kg, ydst=None):
                dma(gv[:, :], gsrc.partition_broadcast(128), [], ["gv"])
                nsub = (NT + 127) // 128
                for ts_ in range(nsub):
                    rows = min(128, NT - ts_ * 128)
                    xin = xr[:rows, ts_, :]
                    act(junk[:rows, :], xin, AF.Square, [f"xr{ts_}"], ["junk", "rs"], accum=rs[:rows, 0:1])
                    act(rs[:rows, 1:2], rs[:rows, 0:1], AF.Ln, ["rs"], ["rs"], bias=epsT[:rows, :], scale=1.0 / D)
                    act(rs[:rows, 2:3], rs[:rows, 1:2], AF.Exp, ["rs"], ["rs"], scale=-0.5)
                    if ydst is not None:
                        stt(xin, xin, rs[:rows, 2:3], gv[:rows, :], ALU.mult, ALU.mult, [f"xr{ts_}", "rs", "gv"], [f"xr{ts_}"])
                        dma(ydst[ts_ * 128:ts_ * 128 + rows, :], xin, [f"xr{ts_}"], [])
                        continue
                    xb = xs[ts_ % 2]
                    stt(xb[:rows, :], xin, rs[:rows, 2:3], gv[:rows, :], ALU.mult, ALU.mult, [f"xr{ts_}", "rs", "gv"], [f"xs{ts_ % 2}"])
                    for k0 in range(0, KC, 8):
                        nk = min(8, KC - k0)
                        for k in range(nk):
                            tp(psT[:, k, 0:rows], xb[:rows, (k0 + k) * 128:(k0 + k + 1) * 128], identB[:rows, :rows],
                               [f"xs{ts_ % 2}", "cB"], ["psT"])
                        cp("act" if (k0 // 8) % 2 == 0 else "dve", xT[:, k0:k0 + nk, ts_ * 128:ts_ * 128 + rows], psT[:, 0:nk, 0:rows],
                           ["psT"], ["xT"])

            def token_phase(NT, xsrc, ydst, prompt_tile):
                nsub = (NT + 127) // 128
                if prompt_tile is None:
                    dma(xr[:NT, 0, :], xsrc, [], ["xr0"])
                    for r in range(4):
                        dma(big[:, 4 * r:4 * r + 2, 0:NT], hs_d[:, 2 * r:2 * r + 2, :], [], [f"big{4 * r}", f"big{4 * r + 1}"])
                        dma(big[:, 4 * r + 2:4 * r + 4, 0:NT], hs_d[:, 8 + 2 * r:10 + 2 * r, :], [], [f"big{4 * r + 2}", f"big{4 * r + 3}"])
                else:
                    t0 = prompt_tile * 512
                    for ts_ in range(4):
                        dma(xr[:, ts_, :], xsrc[t0 + ts_ * 128:t0 + (ts_ + 1) * 128, :], [], [f"xr{ts_}"])
                    qn = "pool" if prompt_tile < (c.NT2 + 1) // 2 else "act"
                    for r in range(4):
                        def issue(e, t0=t0, r=r, qn=qn):
                            if qn not in ctr:
                                pid = e.partition_id()
                                ctr[qn] = (e.compute_val(pid // 4), e.compute_val((pid % 4) * c.Q))
                            gsel, qoff = ctr[qn]
                            src = xg5[bass.ds(gsel, 1), :, r, :, bass.ds(qoff + t0, 512)]
                            return e.dma_start(out=big[:, 4 * r:4 * r + 4, :], in_=src)
                        A(qn, issue, [], [f"big{4 * r + k}" for k in range(4)], dma=True)
                norm_t(NT, nmix, "gmix")
                hkeys = [f"big{k}" for k in range(16)]
                for m in range(KC):
                    sl, ks = load_slab(c.oG + m, (2 * KC + 16) * 128)
                    slv = sl[:, 0:(2 * KC + 16) * 128].rearrange("p (j n) -> p j n", n=128)
                    for k in range(KC):
                        mm(psX[0][:, 0:NT], slv[:, k, :], xT[:, k, 0:NT], k == 0, k == KC - 1, [ks, "xT"], ["psX0"])
                    for k in range(KC):
                        mm(psX[1][:, 0:NT], slv[:, KC + k, :], xT[:, k, 0:NT], k == 0, k == KC - 1, [ks, "xT"], ["psX1"])
                    for k in range(8):
                        ca = 4 * (k // 2) + (k % 2)
                        mm(psX[2][:, 0:NT], slv[:, 2 * KC + k, :], hT(ca)[:, 0:NT], k == 0, k == 7, [ks, f"big{ca}"], ["psX2"])
                    for k in range(8):
                        cb = 4 * (k // 2) + 2 + (k % 2)
                        mm(psX[3][:, 0:NT], slv[:, 2 * KC + 8 + k, :], hT(cb)[:, 0:NT], k == 0, k == 7, [ks, f"big{cb}"], ["psX3"])
                    for i in range(2):
                        act(tmp[i][:, 0:NT], psX[i][:, 0:NT], AF.Exp, [f"psX{i}"], [f"tmp{i}"], scale=-1.0)
                        ts("pool", tmp[i][:, 0:NT], tmp[i][:, 0:NT], 1.0, None, ALU.add, None, [f"tmp{i}"], [f"tmp{i}"])
                        recip(tmp[i][:, 0:NT], tmp[i][:, 0:NT], [f"tmp{i}"], [f"tmp{i}"])
                        tt("dve", tmp[2 + i][:, 0:NT], psX[2 + i][:, 0:NT], tmp[i][:, 0:NT], ALU.mult, [f"psX{2 + i}", f"tmp{i}"], [f"tmp{2 + i}"])
                    tt("pool", mT(m)[:, 0:NT], tmp[2][:, 0:NT], tmp[3][:, 0:NT], ALU.add, ["tmp2", "tmp3"], [f"big{16 + m}"])
                yi = 0
                for g in range(NG):
                    sl, ks = load_slab(c.oO + g, KC * DG)
                    slv = sl[:, 0:KC * DG].rearrange("p (k n) -> p k n", k=KC)
                    for ts_ in range(nsub):
                        rows = min(128, NT - ts_ * 128)
                        py = psY[yi % 2]
                        for k in range(KC):
                            mm(py[:rows, 0:DG], mT(k)[:, ts_ * 128:ts_ * 128 + rows], slv[:, k, :], k == 0, k == KC - 1,
                               [f"big{16 + k}", ks], [f"psY{yi % 2}"])
                        tt("dve", xr[:rows, ts_, g * DG:(g + 1) * DG], xr[:rows, ts_, g * DG:(g + 1) * DG], py[:rows, 0:DG], ALU.add,
                           [f"psY{yi % 2}", f"xr{ts_}"], [f"xr{ts_}"])
                        yi += 1
                norm_t(NT, nffn, "gffn")
                for fg in range(c.NUG):
                    sl, ks = load_slab(c.oU + fg, KC * 512)
                    slv = sl[:, 0:KC * 512].rearrange("p (k n) -> p k n", k=KC)
                    for fc in range(4):
                        py = psY[yi % 2]
                        for k in range(KC):
                            mm(py[:, 0:NT], slv[:, k, fc * 128:(fc + 1) * 128], xT[:, k, 0:NT], k == 0, k == KC - 1, [ks, "xT"], [f"psY{yi % 2}"])
                        tb = tmp[yi % 2]
                        act(tb[:, 0:NT], py[:, 0:NT], AF.Relu, [f"psY{yi % 2}"], [f"tmp{yi % 2}"])
                        f = fg * 4 + fc
                        tt("pool" if yi % 2 == 0 else "dve", big[:, f, 0:NT], tb[:, 0:NT], tb[:, 0:NT], ALU.mult, [f"tmp{yi % 2}"], [f"big{f}"])
                        yi += 1
                for g in range(NG):
                    for fs in range(c.NFS):
                        sl, ks = load_slab(c.oD + g * c.NFS + fs, c.FS * DG)
                        slv = sl[:, 0:c.FS * DG].rearrange("p (k n) -> p k n", k=c.FS)
                        for ts_ in range(nsub):
                            rows = min(128, NT - ts_ * 128)
                            for fc in range(c.FS):
                                f = fs * c.FS + fc
                                mm(psX[ts_][:rows, 0:DG], big[:, f, ts_ * 128:ts_ * 128 + rows], slv[:, fc, :],
                                   fs == 0 and fc == 0, fs == c.NFS - 1 and fc == c.FS - 1, [f"big{f}", ks], [f"psX{ts_}"])
                    for ts_ in range(nsub):
                        rows = min(128, NT - ts_ * 128)
                        tt("dve", xr[:rows, ts_, g * DG:(g + 1) * DG], xr[:rows, ts_, g * DG:(g + 1) * DG], psX[ts_][:rows, 0:DG], ALU.add,
                           [f"psX{ts_}", f"xr{ts_}"], [f"xr{ts_}"])
                norm_t(NT, nfin, "gfin", ydst=ydst)

            for t2 in range(c.NT2):
                token_phase(512, x_q, y_q[t2 * 512:(t2 + 1) * 512, :], t2)
            token_phase(NQ, x_s, y_s, None)
            P.barrier()
            P.emit(block)
    return nc


def _consts():
    i = np.eye(128, dtype=np.float32)
    t = np.triu(np.ones((128, 128), dtype=np.float32))
    o = np.ones((128, 128), dtype=np.float32)
    return np.ascontiguousarray(np.concatenate([i, t, o], axis=1))


def _pk(a, kc):
    n = a.shape[1]
    return np.ascontiguousarray(a.reshape(kc, 128, n).transpose(1, 0, 2))


def build_slabs(cfg, w_in, w_a, w_b, w_out, w_up, w_down):
    c = cfg
    D, KC = c.D, c.KC
    W = np.zeros((8 * c.SPC, 128, c.SLABW), dtype=np.float32)
    oga, ogb = NMIX, NMIX + D
    for m in range(KC):
        parts = [_pk(w_in[:, oga + m * 128:oga + (m + 1) * 128], KC), _pk(w_in[:, ogb + m * 128:ogb + (m + 1) * 128], KC),
                 _pk(w_a[:, m * 128:(m + 1) * 128], 8), _pk(w_b[:, m * 128:(m + 1) * 128], 8)]
        blk = np.concatenate(parts, axis=1).reshape(128, -1)
        W[c.oG + m, :, :blk.shape[1]] = blk
    for g in range(c.NG):
        blk = _pk(w_out[:, g * c.DG:(g + 1) * c.DG], KC).reshape(128, -1)
        W[c.oO + g, :, :blk.shape[1]] = blk
    for fg in range(c.NUG):
        blk = _pk(w_up[:, fg * 512:(fg + 1) * 512], KC).reshape(128, -1)
        W[c.oU + fg, :, :blk.shape[1]] = blk
    for g in range(c.NG):
        for fs in range(c.NFS):
            rows = w_down[fs * c.FS * 128:(fs + 1) * c.FS * 128, g * c.DG:(g + 1) * c.DG]
            blk = _pk(rows, c.FS).reshape(128, -1)
            W[c.oD + g * c.NFS + fs, :, :blk.shape[1]] = blk
    o = np.cumsum([0, MW, MW, MW, MW, MH, MH, FW, FW, FW, FH])
    mq, mk, mv, mo, mi, mf, fq, fk, fv, ff = [w_in[:, o[i]:o[i + 1]] for i in range(10)]
    perm = np.concatenate([mq, mk, mv, mo, fq, fk, fv, mi, mf, ff], axis=1)
    Ws = np.zeros((c.NSS, 128, KC * 512), dtype=np.float32)
    for si in range(c.NSS):
        cols = perm[:, si * 512:(si + 1) * 512]
        blk = np.zeros((128, KC, 512), dtype=np.float32)
        blk[:, :, :cols.shape[1]] = _pk(cols, KC)
        Ws[si] = blk.reshape(128, -1)
    return W, Ws


def pack_inputs(cfg, x_prompt, x_sample, cache_fox_k, cache_fox_v, cache_fox_logf, state_mlstm_c, state_mlstm_n,
                state_mlstm_m, norm_mix, w_in, b_mlstm_i, b_mlstm_f, b_fox_f, norm_mlstm_h, w_branch_a,
                w_branch_b, w_out, norm_ffn, w_up, w_down, norm_final):
    c = cfg
    f = lambda a: np.asarray(a, dtype=np.float32)
    x_prompt, x_sample = f(x_prompt), f(x_sample)
    w_in0 = f(w_in)[0]
    W, Ws = build_slabs(c, w_in0, f(w_branch_a)[0], f(w_branch_b)[0], f(w_out)[0], f(w_up)[0], f(w_down)[0])
    o = np.cumsum([0, MW, MW, MW, MW, MH, MH, FW, FW, FW, FH])
    mq, mk, mv, mo, mi, mf, fq, fk, fv, ff = [w_in0[:, o[i]:o[i + 1]] for i in range(10)]
    cst = _consts()
    nh = f(norm_mlstm_h)[0]
    maps = []
    for core in range(8):
        b, r = core // 4, core % 4
        hA, hB = 2 * r, 2 * r + 1
        sl = lambda a, h, w: a[:, h * w:(h + 1) * w]
        w1 = np.concatenate([
            sl(mq, r, 256), sl(mk, r, 256), sl(fq, hA, 128), sl(fq, hB, 128), sl(fk, hA, 128), sl(fk, hB, 128),
            sl(mk, r, 256), sl(mv, r, 256),
            sl(mo, r, 256), mi[:, r:r + 1], mf[:, r:r + 1], ff[:, hA:hA + 1], ff[:, hB:hB + 1],
            sl(fk, hA, 128), sl(fv, hA, 128), sl(fk, hB, 128), sl(fv, hB, 128)], axis=1)
        bias4 = np.array([[f(b_mlstm_i)[0, r], f(b_mlstm_f)[0, r], f(b_fox_f)[0, hA], f(b_fox_f)[0, hB]]], dtype=np.float32)
        bias_s = np.concatenate([f(b_mlstm_i)[0], f(b_mlstm_f)[0], f(b_fox_f)[0]])[None, :]
        maps.append({
            "x_b": np.ascontiguousarray(x_prompt[b]),
            "x_q": np.ascontiguousarray(x_prompt[b, r * c.Q:(r + 1) * c.Q]),
            "x_s": np.ascontiguousarray(x_sample[core]),
            "w1": np.ascontiguousarray(w1),
            "w2s": np.ascontiguousarray(W[core * c.SPC:(core + 1) * c.SPC].reshape(c.SPC * 128, c.SLABW)),
            "wss": Ws.reshape(c.NSS * 128, c.KC * 512),
            "bias4": bias4,
            "bias_s": np.ascontiguousarray(bias_s.astype(np.float32)),
            "nmix": f(norm_mix)[0][None, :], "nffn": f(norm_ffn)[0][None, :], "nfin": f(norm_final)[None, :],
            "normh_c": np.ascontiguousarray(nh[r * 256:(r + 1) * 256].reshape(2, 128).T),
            "normh_s": np.ascontiguousarray(nh.reshape(8, 128).T),
            "ck_s": np.ascontiguousarray(f(cache_fox_k)[0, core].reshape(c.PAST, FW)),
            "cv_s": np.ascontiguousarray(f(cache_fox_v)[0, core].reshape(c.PAST, FW)),
            "clf_s": np.ascontiguousarray(f(cache_fox_logf)[0, core]),
            "mc_s": np.ascontiguousarray(f(state_mlstm_c)[0, core]),
            "mn_s": np.ascontiguousarray(f(state_mlstm_n)[0, core].reshape(MH * 2, 128).T),
            "mm_s": np.ascontiguousarray(f(state_mlstm_m)[0, core][None, :]),
            "cst_f": cst,
        })
    return maps


def unpack_outputs(cfg, res):
    c = cfg
    B = 2
    y_prompt = np.zeros((B, c.S, c.D), np.float32)
    y_sample = np.zeros((8, c.DEC, c.D), np.float32)
    pk = np.zeros((1, B, c.S, FH, FD), np.float32)
    pv = np.zeros((1, B, c.S, FH, FD), np.float32)
    plf = np.zeros((1, B, c.S, FH), np.float32)
    pc = np.zeros((1, B, MH, MDK, MDV), np.float32)
    pn = np.zeros((1, B, MH, MDK), np.float32)
    pm = np.zeros((1, B, MH), np.float32)
    sk = np.zeros((1, 8, c.DEC, FH, FD), np.float32)
    sv = np.zeros((1, 8, c.DEC, FH, FD), np.float32)
    slf = np.zeros((1, 8, c.DEC, FH), np.float32)
    sc = np.zeros((1, 8, MH, MDK, MDV), np.float32)
    sn = np.zeros((1, 8, MH, MDK), np.float32)
    smm = np.zeros((1, 8, MH), np.float32)
    for core in range(8):
        o = res[core]
        b, r = core // 4, core % 4
        y_prompt[b, r * c.Q:(r + 1) * c.Q] = o["y_q"]
        y_sample[core] = o["y_s"]
        pk[0, b, :, 2 * r:2 * r + 2, :] = np.asarray(o["o_fk"]).reshape(c.S, 2, FD)
        pv[0, b, :, 2 * r:2 * r + 2, :] = np.asarray(o["o_fv"]).reshape(c.S, 2, FD)
        plf[0, b, :, 2 * r:2 * r + 2] = o["o_flf"]
        pc[0, b, r] = o["o_mc"]
        pn[0, b, r] = np.asarray(o["o_mn"]).T.reshape(MDK)
        pm[0, b, r] = np.asarray(o["o_mm"]).reshape(())
        sk[0, core] = np.asarray(o["o_sfk"]).reshape(c.DEC, FH, FD)
        sv[0, core] = np.asarray(o["o_sfv"]).reshape(c.DEC, FH, FD)
        slf[0, core] = o["o_sflf"]
        sc[0, core] = o["o_smc"]
        sn[0, core] = np.asarray(o["o_smn"]).T.reshape(MH, MDK)
        smm[0, core] = np.asarray(o["o_smm"]).reshape(MH)
    return (y_prompt, y_sample, pk, pv, plf, pc, pn, pm, sk, sv, slf, sc, sn, smm)


def kernel(**inputs):
    cfg = FULL
    maps = pack_inputs(cfg, **inputs)
    nc = build_program(cfg)
    res = run_bass_kernel_spmd(nc, maps, core_ids=list(range(8)))
    return unpack_outputs(cfg, res.results)
```

```python
import math
import os
from contextlib import ExitStack

import numpy as np
import concourse.bass as bass
import concourse.mybir as mybir
from concourse.bass_utils import run_bass_kernel_spmd

F32 = mybir.dt.float32
BF16 = mybir.dt.bfloat16
AF = mybir.ActivationFunctionType
ALU = mybir.AluOpType
AX = mybir.AxisListType

EPS = 1e-6
MH, MDK, MDV = 4, 256, 256
FH, FD = 8, 128
MW = MH * MDV
FW = FH * FD
NMIX = 4 * MW + 2 * MH + 3 * FW + FH


class Cfg:
    def __init__(s, D=2048, S=16384, DFF=8192, PAST=2048, DEC=32):
        s.D, s.S, s.DFF, s.PAST, s.DEC = D, S, DFF, PAST, DEC
        s.KC = D // 128
        s.Q = S // 4
        s.NT1 = S // 512
        s.NT2 = s.Q // 512
        s.NB = S // 128
        s.DG = min(512, D)
        s.NG = D // s.DG
        s.FCH = DFF // 128
        s.FS = min(16, s.FCH)
        s.NFS = s.FCH // s.FS
        s.NUG = DFF // 512
        s.NSS = 15
        s.oG = 0
        s.oO = s.oG + s.KC
        s.oU = s.oO + s.NG
        s.oD = s.oU + s.NUG
        s.oS = s.oD + s.NG * s.NFS
        s.NSLAB = s.oS
        s.SPC = (s.NSLAB + 7) // 8
        s.SLABW = max((2 * s.KC + 16) * 128, s.KC * 512, s.FS * s.DG)
        s.D_IN = NMIX + 2 * D
        s.PB = PAST // 128


FULL = Cfg()


class Op:
    __slots__ = ("eng", "fn", "deps", "signal", "sem", "val", "dma", "seq")

    def __init__(self, eng, fn, dma):
        self.eng, self.fn, self.dma = eng, fn, dma
        self.deps = []
        self.signal = False
        self.sem = None
        self.val = None
        self.seq = 0


ENGS = ("pe", "act", "dve", "pool", "sp")


class Prog:
    NDMA = {"sp": 10, "pool": 4, "act": 2}

    def __init__(self, nc, stack):
        self.nc = nc
        self.esem = {e: stack.enter_context(nc.semaphore("es_" + e)) for e in ENGS}
        self.dsem = {q: [stack.enter_context(nc.semaphore(f"ds_{q}{i}")) for i in range(n)]
                     for q, n in self.NDMA.items()}
        self.ccsem = stack.enter_context(nc.semaphore("ccs"))
        self.ncc = 0
        self.ecount = {e: 0 for e in ENGS}
        self.dcount = {q: 0 for q in self.NDMA}
        self.dlast = {q: [None] * n for q, n in self.NDMA.items()}
        self.last_w = {}
        self.readers = {}
        self.waited = {e: {} for e in ENGS}
        self.ops = {e: [] for e in ENGS}
        self.all_dma = []
        self.seq = 0

    def add(self, eng, fn, r=(), w=(), dma=False, cc=False):
        op = Op(eng, fn, dma or cc)
        deps = []
        for k in r:
            lw = self.last_w.get(k)
            if lw is not None:
                deps.append(lw)
        for k in w:
            lw = self.last_w.get(k)
            if lw is not None:
                deps.append(lw)
            deps.extend(self.readers.get(k, ()))
        if dma:
            i = self.dcount[eng] % self.NDMA[eng]
            prev = self.dlast[eng][i]
            if prev is not None:
                deps.append(prev)
            op.sem = self.dsem[eng][i]
            op.val = 16 * (self.dcount[eng] // self.NDMA[eng] + 1)
            self.dcount[eng] += 1
            self.dlast[eng][i] = op
            self.all_dma.append(op)
        elif cc:
            self.ncc += 1
            op.sem = self.ccsem
            op.val = self.ncc
            self.all_dma.append(op)
        self.seq += 1
        op.seq = self.seq
        best = {}
        for d in deps:
            if d is op:
                continue
            if eng == "pe" and d.eng == "pe" and not d.dma:
                continue
            k = id(d.sem) if d.dma else d.eng
            if k not in best or best[k].seq < d.seq:
                best[k] = d
        for d in best.values():
            if not d.dma:
                d.signal = True
            op.deps.append(d)
        for k in r:
            self.readers.setdefault(k, []).append(op)
        for k in w:
            self.last_w[k] = op
            self.readers[k] = []
        self.ops[eng].append(op)
        return op

    def barrier(self, keep_cc=False):
        lasts = []
        for e in ENGS:
            for o in reversed(self.ops[e]):
                if not o.dma and o.fn is not None:
                    o.signal = True
                    lasts.append(o)
                    break
        best = {}
        kept = []
        for o in self.all_dma:
            if keep_cc and o.sem is self.ccsem:
                kept.append(o)
                continue
            k = id(o.sem)
            if k not in best or best[k].val < o.val:
                best[k] = o
        dm = list(best.values())
        self.all_dma = kept
        for e in ENGS:
            op = Op(e, None, False)
            op.deps = list(lasts) + dm
            self.ops[e].append(op)
        self.last_w = {}
        self.readers = {}

    def emit(self, block):
        nc = self.nc
        for e in ENGS:
            for o in self.ops[e]:
                if o.dma:
                    continue
                if o.signal:
                    self.ecount[e] += 1
                    o.sem = self.esem[e]
                    o.val = self.ecount[e]
        prog = self

        def body(ename):
            def f(eng):
                waited = prog.waited[ename]
                for o in prog.ops[ename]:
                    for d in o.deps:
                        if waited.get(id(d.sem), -1) >= d.val:
                            continue
                        eng.wait_ge(d.sem, d.val)
                        waited[id(d.sem)] = d.val
                    if o.fn is None:
                        continue
                    ins = o.fn(eng)
                    if o.dma:
                        if o.sem is prog.ccsem:
                            ins.then_inc(o.sem)
                        else:
                            ins.then_inc(o.sem, 16)
                    elif o.signal:
                        ins.then_inc(o.sem, 1)
                prog.ops[ename] = []
            return f

        block.tensor(body("pe"))
        block.scalar(body("act"))
        block.vector(body("dve"))
        block.gpsimd(body("pool"))
        block.sync(body("sp"))


def build_program(cfg):
    c = cfg
    D, S, KC, DG, NG = c.D, c.S, c.KC, c.DG, c.NG
    nc = bass.Bass("TRN2", target_bir_lowering=False)
    dt = nc.dram_tensor

    def din(name, shape, dtype=F32):
        return dt(name, list(shape), dtype, kind="ExternalInput").ap()

    def dout(name, shape, dtype=F32):
        return dt(name, list(shape), dtype, kind="ExternalOutput").ap()

    def dscr(name, shape, dtype=BF16):
        return dt(name, list(shape), dtype, kind="Internal").ap()

    x_b = din("x_b", [S, D])
    x_q = din("x_q", [c.Q, D])
    x_s = din("x_s", [c.DEC, D])
    w1 = din("w1", [D, 2308])
    w2s = din("w2s", [c.SPC * 128, c.SLABW])
    wss = din("wss", [c.NSS * 128, c.KC * 512])
    bias4 = din("bias4", [1, 4])
    bias_s = din("bias_s", [1, 16])
    nmix = din("nmix", [1, D])
    nffn = din("nffn", [1, D])
    nfin = din("nfin", [1, D])
    normh_c = din("normh_c", [128, 2])
    normh_s = din("normh_s", [128, 8])
    ck_s = din("ck_s", [c.PAST, FH * FD])
    cv_s = din("cv_s", [c.PAST, FH * FD])
    clf_s = din("clf_s", [c.PAST, FH])
    mc_s = din("mc_s", [MH, MDK, MDV])
    mn_s = din("mn_s", [128, MH * 2])
    mm_s = din("mm_s", [1, MH])
    cst_f = din("cst_f", [128, 3 * 128])
    y_q = dout("y_q", [c.Q, D])
    y_s = dout("y_s", [c.DEC, D])
    o_fk = dout("o_fk", [S, 2 * FD])
    o_fv = dout("o_fv", [S, 2 * FD])
    o_flf = dout("o_flf", [S, 2])
    o_mc = dout("o_mc", [MDK, MDV])
    o_mn = dout("o_mn", [128, 2])
    o_mm = dout("o_mm", [1, 1])
    o_sfk = dout("o_sfk", [c.DEC, FW])
    o_sfv = dout("o_sfv", [c.DEC, FW])
    o_sflf = dout("o_sflf", [c.DEC, FH])
    o_smc = dout("o_smc", [MH, MDK, MDV])
    o_smn = dout("o_smn", [128, MH * 2])
    o_smm = dout("o_smm", [1, MH])
    NW = c.SPC * 128 * c.SLABW // S
    assert NW * S == c.SPC * 128 * c.SLABW
    cat = dscr("cat", [NW + 512, S])
    G = dscr("G", [8 * (NW + 512), S])
    W2l = cat[0:NW, :].rearrange("a b -> (a b)").rearrange("(r w) -> r w", w=c.SLABW)
    Gv = G.rearrange("(c m) s -> c m s", c=8)

    def W2v(idx):
        rk, j = idx // c.SPC, idx % c.SPC
        blk = Gv[rk, 0:NW, :].rearrange("a b -> (a b)").rearrange("(r w) -> r w", w=c.SLABW)
        return blk[j * 128:(j + 1) * 128, :]
    qT_d = dscr("qT_d", [2, 128, S])
    kT_d = dscr("kT_d", [2, 128, S])
    v_d = dscr("v_d", [2, 128, c.NB, 128])
    xbuf = cat[NW:NW + 512, :]
    hs_d = dscr("hs_d", [128, 16, c.DEC])
    Wsl = dscr("Wsl", [c.NSS * 128, c.KC * 512])

    SCALE = FD ** -0.5

    with ExitStack() as top:
        P = Prog(nc, top)
        A = P.add

        def mm(out, lhsT, rhs, st, sp, r, w):
            A("pe", lambda e: e.matmul(out, lhsT, rhs, start=st, stop=sp), r, w)

        def tp(out, in_, ident, r, w):
            A("pe", lambda e: e.transpose(out, in_, ident), r, w)

        def act(out, in_, func, r, w, bias=None, scale=None, accum=None):
            kw = {}
            if bias is not None:
                kw["bias"] = bias
            if scale is not None:
                kw["scale"] = scale
            if accum is not None:
                kw["accum_out"] = accum
            A("act", lambda e: e.activation(out=out, in_=in_, func=func, **kw), r, w)

        def ts(eng, out, in0, s1, s2, op0, op1, r, w):
            if s2 is None and eng == "pool":
                if op0 == ALU.add:
                    s2, op1 = 1.0, ALU.mult
                elif op0 == ALU.mult:
                    s2, op1 = 0.0, ALU.add
            if s2 is None:
                A(eng, lambda e: e.tensor_scalar(out=out, in0=in0, scalar1=s1, scalar2=None, op0=op0), r, w)
            else:
                A(eng, lambda e: e.tensor_scalar(out=out, in0=in0, scalar1=s1, scalar2=s2, op0=op0, op1=op1), r, w)

        def tt(eng, out, in0, in1, op, r, w):
            A(eng, lambda e: e.tensor_tensor(out=out, in0=in0, in1=in1, op=op), r, w)

        def stt(out, in0, s, in1, op0, op1, r, w):
            A("dve", lambda e: e.scalar_tensor_tensor(out=out, in0=in0, scalar=s, in1=in1, op0=op0, op1=op1), r, w)

        def recip(out, in_, r, w):
            A("dve", lambda e: e.reciprocal(out=out, in_=in_), r, w)

        def cp(eng, out, in_, r, w):
            if eng == "act":
                act(out, in_, AF.Copy, r, w)
            else:
                A(eng, lambda e: e.tensor_copy(out=out, in_=in_), r, w)

        def dma(out, in_, r, w, q="sp"):
            A(q, lambda e: e.dma_start(out=out, in_=in_), r, w, dma=True)

        sb = lambda st, name, shape, dtp: st.enter_context(nc.sbuf_tensor(name, list(shape), dtp))
        ps = lambda st, name, shape, dtp: st.enter_context(nc.psum_tensor(name, list(shape), dtp))

        cF = sb(top, "cF", [128, 384], F32)
        cB = sb(top, "cB", [128, 384], BF16)
        identF, triF, onesF = cF[:, 0:128], cF[:, 128:256], cF[:, 256:384]
        identB, triB, onesB = cB[:, 0:128], cB[:, 128:256], cB[:, 256:384]
        epsT = sb(top, "epsT", [128, 1], F32)
        Pk = sb(top, "Pk", [128, 2, c.NB], F32)
        Pend = sb(top, "Pend", [128, 2, c.NT1], F32)

        with ExitStack() as st, nc.Block() as block:
            dma(cF[:], cst_f, [], ["cF"])
            cp("dve", cB[:], cF[:], ["cF"], ["cB"])
            A("dve", lambda e: e.memset(epsT[:], EPS), [], ["eps"])
            P.barrier()

            w1fm = sb(st, "w1fm", [128, KC, 1024], BF16)
            w1tm = sb(st, "w1tm", [128, KC, 1284], BF16)
            w1v = w1.rearrange("(k p) n -> p k n", p=128)
            for k in range(KC):
                dma(w1fm[:, k, :], w1v[:, k, 0:1024], [], ["w1"], q="pool")
                dma(w1tm[:, k, :], w1v[:, k, 1024:2308], [], ["w1"], q="pool")

            for j in range(c.SPC):
                dma(W2l[j * 128:(j + 1) * 128, :], w2s[j * 128:(j + 1) * 128, :], [], [f"W2l{j}"], q="pool")

            for j in range(c.NSS):
                dma(Wsl[j * 128:(j + 1) * 128, :], wss[j * 128:(j + 1) * 128, :], [], [], q="pool")
            grep = sb(st, "grep", [128, D], F32)
            dma(grep[:], nmix.partition_broadcast(128), [], ["grep"])
            b4 = sb(st, "b4", [128, 4], F32)
            dma(b4[:], bias4.partition_broadcast(128), [], ["b4"])
            nh = sb(st, "nh", [128, 2], F32)
            dma(nh[:], normh_c, [], ["nh"])

            xa = [sb(st, f"xa{i}", [128, D], F32) for i in range(3)]
            junk = sb(st, "junk", [128, D], BF16)
            xs = [sb(st, f"xs{i}", [128, D], BF16) for i in range(2)]
            xT = [sb(st, f"xT{i}", [128, KC, 512], BF16) for i in range(2)]
            qT2 = [sb(st, f"qT{i}", [128, 2, 512], BF16) for i in range(2)]
            kT2 = [sb(st, f"kT{i}", [128, 2, 512], BF16) for i in range(2)]
            fst = [sb(st, f"fst{i}", [128, 4, 512], BF16) for i in range(2)]
            ktok2 = [sb(st, f"ktok{i}", [128, 4, 256], BF16) for i in range(2)]
            vext2 = [sb(st, f"vext{i}", [128, 4, 257], BF16) for i in range(2)]
            sigo2 = [sb(st, f"sigo{i}", [128, 4, 256], F32) for i in range(2)]
            cst32 = [sb(st, f"cst32_{i}", [128, 512], F32) for i in range(2)]
            vbf = [sb(st, f"vbf{i}", [128, 2, 128], BF16) for i in range(2)]
            g42 = [sb(st, f"g4_{i}", [128, 4, 4], F32) for i in range(2)]
            e3 = sb(st, "e3", [128, 4, 3], F32)
            sp3 = sb(st, "sp3", [128, 4, 3], F32)
            lfo = [sb(st, f"lfo{i}", [128, 2], F32) for i in range(2)]
            carry = sb(st, "carry", [128, 2], F32)
            sm = sb(st, "sm", [128, 4, 16], F32)
            m0 = sb(st, "m0", [128, 1], F32)
            cS = sb(st, "cS", [128, 2, 256], F32)
            nS = sb(st, "nS", [128, 2], F32)
            cbf = sb(st, "cbf", [128, 2, 257], BF16)
            diag = sb(st, "diag", [128, 128], F32)
            stT = [sb(st, f"stT{i}", [128, 128], BF16) for i in range(2)]
            ku = [sb(st, f"ku{i}", [128, 256], BF16) for i in range(2)]
            hout = [sb(st, f"hout{i}", [128, 256], BF16) for i in range(2)]
            haT = [sb(st, f"haT{i}", [128, 2, 512], BF16) for i in range(2)]
            rs = sb(st, "rs", [128, 8], F32)

            psT = ps(st, "psT", [128, 8, 128], BF16)
            psF = [ps(st, f"psF{i}", [128, 512], F32) for i in range(2)]
            psA = ps(st, "psA", [128, 512], F32)
            psB = ps(st, "psB", [128, 512], F32)
            psC = ps(st, "psC", [128, 512], F32)
            psM = ps(st, "psM", [128, 512], F32)
            psK = ps(st, "psK", [128, 2, 256], F32)

            A("dve", lambda e: e.memset(carry[:], 0.0), [], ["carry"])
            A("dve", lambda e: e.memset(m0[:], 0.0), [], ["m0"])
            A("dve", lambda e: e.memset(cS[:], 0.0), [], ["cS"])
            A("dve", lambda e: e.memset(nS[:], 0.0), [], ["nS"])
            A("dve", lambda e: e.memset(cbf[:], 0.0), [], ["cbf"])
            for pp in range(2):
                for j in range(4):
                    A("dve", lambda e, j=j, pp=pp: e.memset(vext2[pp][:, j, 256:257], 1.0), [], [f"vext{pp}.{j}"])

            def norm_cast(xin, xout, gvec, rows, kx, kr, ky, kg):
                act(junk[:rows, :], xin, AF.Square, [kx], ["junk", kr], accum=rs[:rows, 0:1])
                act(rs[:rows, 1:2], rs[:rows, 0:1], AF.Ln, [kr], [kr], bias=epsT[:rows, :], scale=1.0 / D)
                act(rs[:rows, 2:3], rs[:rows, 1:2], AF.Exp, [kr], [kr], scale=-0.5)
                stt(xout, xin, rs[:rows, 2:3], gvec, ALU.mult, ALU.mult, [kx, kr, kg], [ky])

            SMW = 16

            xv = x_b.rearrange("(n p) d -> n p d", p=128)
            xbv = xbuf.rearrange("(k p) s -> p k s", p=128)

            state = {"gen": None}

            def tick():
                g = state["gen"]
                if g is not None:
                    try:
                        next(g)
                    except StopIteration:
                        state["gen"] = None

            def drain():
                while state["gen"] is not None:
                    tick()

            def tile_chunks(ti):
                pp = ti % 2
                L = 128
                qT, kT, ktok, vext, sigo, g4 = qT2[pp], kT2[pp], ktok2[pp], vext2[pp], sigo2[pp], g42[pp]
                hb_ = haT[ti % 2]
                for j in range(4):
                    kj = f"{pp}.{j}"
                    ks = f"sm{j}"
                    smj = sm[:, j, :]
                    blk = ti * 4 + j
                    act(e3[:L, j, :], g4[:L, j, 1:4], AF.Exp, ["g4" + kj], ["e3" + f"{j}"], scale=-1.0)
                    act(sp3[:L, j, :], e3[:L, j, :], AF.Ln, ["e3" + f"{j}"], ["sp3" + f"{j}"], bias=1.0)
                    yield
                    cum = psM[:L, 392:395]
                    tot = psM[:, 396:399]
                    mm(cum, triF[:L, :L], sp3[:L, j, :], True, True, ["sp3" + f"{j}"], ["psM"])
                    mm(tot, onesF[:L, :], sp3[:L, j, :], True, True, ["sp3" + f"{j}"], ["psM"])
                    yield
                    for h in range(2):
                        ts("dve", Pk[:L, h, blk:blk + 1], cum[:, 1 + h:2 + h], carry[:L, h:h + 1], None, ALU.add, None,
                           ["psM", "carry"], ["Pk"])
                    tt("dve", carry[:, :], carry[:, :], tot[:, 1:3], ALU.add, ["psM", "carry"], ["carry"])
                    lf = lfo[blk % 2]
                    ts("dve", lf[:L, :], sp3[:L, j, 1:3], -1.0, None, ALU.mult, None, ["sp3" + f"{j}"], [f"lfo{blk % 2}"])
                    dma(o_flf[blk * 128:blk * 128 + L, :], lf[:L, :], [f"lfo{blk % 2}"], [])
                    tt("dve", smj[:L, 0:1], g4[:L, j, 0:1], cum[:, 0:1], ALU.add, ["g4" + kj, "psM"], [ks])
                    ts("dve", diag[:L, :L], identF[:L, :L], smj[:L, 0:1], None, ALU.mult, None, [ks], ["diag"])
                    cp("dve", smj[:L, 13:14], cum[:, 0:1], ["psM"], [ks])
                    cp("dve", smj[:, 15:16], tot[:, 0:1], ["psM"], [ks])
                    yield
                    arow = psM[:, 0:L]
                    mm(arow, onesF[:L, :], diag[:L, :L], True, True, ["diag"], ["psM"])
                    yield
                    A("dve", lambda e, smj=smj, arow=arow: e.tensor_reduce(out=smj[:, 1:2], in_=arow, axis=AX.X, op=ALU.max),
                      ["psM"], [ks])
                    tt("dve", smj[:, 2:3], smj[:, 1:2], m0[:, :], ALU.max, [ks, "m0"], [ks])
                    ts("dve", smj[:, 3:4], smj[:, 2:3], -1.0, None, ALU.mult, None, [ks], [ks])
                    act(smj[:, 4:5], m0[:, :], AF.Exp, ["m0", ks], [ks], bias=smj[:, 3:4])
                    act(smj[:L, 5:6], smj[:L, 0:1], AF.Exp, [ks], [ks], bias=smj[:L, 3:4])
                    ts("dve", smj[:L, 5:6], smj[:L, 5:6], MDK ** -0.5, None, ALU.mult, None, [ks], [ks])
                    act(smj[:L, 6:7], smj[:L, 13:14], AF.Exp, [ks], [ks], bias=smj[:L, 3:4])
                    tt("dve", m0[:, :], smj[:, 2:3], smj[:, 15:16], ALU.subtract, [ks, "m0"], ["m0"])
                    g0, u, flo = smj[:, 4:5], smj[:L, 5:6], smj[:L, 6:7]
                    kub = ku[j % 2]
                    ts("pool", kub[:L, :], ktok[:, j, :], u, None, ALU.mult, None, ["ktok" + kj, ks], [f"ku{j % 2}"])
                    act(cbf[:, :, 0:256], cS[:, :, :], AF.Copy, ["cS", ks], ["cbf"], scale=g0)
                    act(cbf[:, :, 256:257], nS[:, :].unsqueeze(2), AF.Copy, ["nS", ks], ["cbf"], scale=g0)
                    yield
                    ST = psM[:L, 0:L]
                    for dc in range(2):
                        mm(ST, kT[:, dc, j * 128:(j + 1) * 128], qT[:, dc, j * 128:(j + 1) * 128], dc == 0, dc == 1,
                           [f"qT{pp}.{dc}", f"kT{pp}.{dc}"], ["psM"])
                    yield
                    stb = stT[j % 2]
                    stt(stb[:L, :L], ST, u, triB[:L, :L], ALU.mult, ALU.mult, ["psM", ks], [f"stT{j % 2}"])
                    yield
                    NE = psM[:L, 128:385]
                    mm(NE, stb[:L, :L], vext[:, j, :], True, False, [f"stT{j % 2}", "vext" + kj], ["psM"])
                    for dc in range(2):
                        mm(NE, qT[:, dc, j * 128:(j + 1) * 128], cbf[:, dc, :], False, dc == 1, [f"qT{pp}.{dc}", "cbf"], ["psM"])
                    for dc in range(2):
                        mm(psK[:, dc, :], kub[:L, dc * 128:(dc + 1) * 128], vext[:, j, 0:256], True, True,
                           [f"ku{j % 2}", "vext" + kj], ["psK"])
                        mm(psM[:, 386 + dc:387 + dc], kub[:L, dc * 128:(dc + 1) * 128], vext[:, j, 256:257], True, True,
                           [f"ku{j % 2}", "vext" + kj], ["psM"])
                    yield
                    stt(cS[:, :, :], cS[:, :, :], g0, psK[:, :, :], ALU.mult, ALU.add, ["cS", "psK", ks, "cbf"], ["cS"])
                    stt(nS[:, :], nS[:, :], g0, psM[:, 386:388], ALU.mult, ALU.add, ["nS", "psM", ks, "cbf"], ["nS"])
                    act(smj[:L, 14:15], NE[:, 256:257], AF.Abs, ["psM"], [ks])
                    tt("dve", smj[:L, 7:8], smj[:L, 14:15], flo, ALU.max, [ks], [ks])
                    recip(smj[:L, 8:9], smj[:L, 7:8], [ks], [ks])
                    act(junk[:L, 0:256], NE[:, 0:256], AF.Square, ["psM", ks], ["junk", ks],
                        scale=smj[:L, 8:9], accum=smj[:L, 9:10])
                    act(smj[:L, 10:11], smj[:L, 9:10], AF.Ln, [ks], [ks], bias=epsT[:L, :], scale=1.0 / MDV)
                    act(smj[:L, 11:12], smj[:L, 10:11], AF.Exp, [ks], [ks], scale=-0.5)
                    tt("dve", smj[:L, 12:13], smj[:L, 11:12], smj[:L, 8:9], ALU.mult, [ks], [ks])
                    hob = hout[j % 2]
                    stt(hob[:L, :], NE[:, 0:256], smj[:L, 12:13], sigo[:, j, :], ALU.mult, ALU.mult,
                        ["psM", ks, "sigo" + kj], [f"hout{j % 2}"])
                    yield
                    for dc in range(2):
                        tp(psT[:, 4 + dc, :], hob[:L, dc * 128:(dc + 1) * 128], identB[:L, :L], [f"hout{j % 2}"], ["psT"])
                    yield
                    for dc in range(2):
                        ts("dve", hb_[:, dc, j * 128:(j + 1) * 128], psT[:, 4 + dc, :], nh[:, dc:dc + 1], None, ALU.mult, None,
                           ["psT", "nh"], [f"haT{ti % 2}"])
                    if j == 3:
                        ts("dve", Pend[:, :, ti:ti + 1], carry[:, :].unsqueeze(2), 1.0, None, ALU.mult, None, ["carry"], ["Pend"])
                        dma(xbv[:, 0:2, ti * 512:(ti + 1) * 512], hb_[:, :, :], [f"haT{ti % 2}"], [])
                    yield

            NSUB = c.NT1 * 4

            def st_load(sub):
                dma(xa[sub % 3][:], xv[sub], [], [f"xa{sub % 3}"])

            def st_norm(sub):
                norm_cast(xa[sub % 3][:], xs[sub % 2][:], grep[:], 128, f"xa{sub % 3}", "rs", f"xs{sub % 2}", "grep")

            def st_tp(sub):
                ti, j = sub // 4, sub % 4
                xTt, kxT, xsb = xT[ti % 2], f"xT{ti % 2}", xs[sub % 2]
                for k0 in range(0, KC, 4):
                    nk = min(4, KC - k0)
                    for k in range(nk):
                        tp(psT[:, k, :], xsb[:, (k0 + k) * 128:(k0 + k + 1) * 128], identB, [f"xs{sub % 2}", "cB"], ["psT"])
                    cp("act" if (k0 // 4) % 2 == 0 else "dve", xTt[:, k0:k0 + nk, j * 128:(j + 1) * 128], psT[:, 0:nk, :],
                       ["psT"], [kxT + f".{j}.{k0 // 4}"])

            def st_mm(sub):
                ti, j = sub // 4, sub % 4
                xTt, kxT = xT[ti % 2], f"xT{ti % 2}"
                for k in range(KC):
                    l = xTt[:, k, j * 128:(j + 1) * 128]
                    kk = [kxT + f".{j}.{k // 4}", "w1"]
                    mm(psA[:, :], l, w1tm[:, k, 0:512], k == 0, k == KC - 1, kk, ["psA"])
                    mm(psB[:, 0:260], l, w1tm[:, k, 512:772], k == 0, k == KC - 1, kk, ["psB"])
                    mm(psC[:, :], l, w1tm[:, k, 772:1284], k == 0, k == KC - 1, kk, ["psC"])
                    if k % 4 == 3:
                        tick()

            def st_evac(sub):
                ti, j = sub // 4, sub % 4
                pp = ti % 2
                kj = f"{pp}.{j}"
                ktok, vext, sigo, g4 = ktok2[pp], vext2[pp], sigo2[pp], g42[pp]
                cp("act", ktok[:, j, :], psA[:, 0:256], ["psA"], ["ktok" + kj])
                cp("act", vext[:, j, 0:256], psA[:, 256:512], ["psA"], ["vext" + kj])
                act(sigo[:, j, :], psB[:, 0:256], AF.Exp, ["psB"], ["sigo" + kj], scale=-1.0)
                ts("pool", sigo[:, j, :], sigo[:, j, :], 1.0, None, ALU.add, None, ["sigo" + kj], ["sigo" + kj])
                recip(sigo[:, j, :], sigo[:, j, :], ["sigo" + kj], ["sigo" + kj])
                tt("dve", g4[:, j, :], psB[:, 256:260], b4[:, :], ALU.add, ["psB", "b4"], ["g4" + kj])
                c32 = cst32[sub % 2]
                cp("dve", c32[:, :], psC[:, :], ["psC"], [f"c32_{sub % 2}"])
                c3v = c32[:, :].rearrange("p (h t d) -> p h t d", h=2, t=2)
                dma(o_fk[sub * 128:(sub + 1) * 128, :].rearrange("p (h d) -> p h d", h=2), c3v[:, :, 0, :],
                    [f"c32_{sub % 2}"], [])
                dma(o_fv[sub * 128:(sub + 1) * 128, :].rearrange("p (h d) -> p h d", h=2), c3v[:, :, 1, :],
                    [f"c32_{sub % 2}"], [])
                vb = vbf[sub % 2]
                cp("pool", vb[:, :, :], c3v[:, :, 1, :], [f"c32_{sub % 2}"], [f"vbf{sub % 2}"])
                dma(v_d[:, :, sub, :].rearrange("h p d -> p h d"), vb[:, :, :], [f"vbf{sub % 2}"], [])

            st_load(0)
            if NSUB > 1:
                st_load(1)
            st_norm(0)
            st_tp(0)
            for ti in range(c.NT1):
                xTt = xT[ti % 2]
                kxT = f"xT{ti % 2}"
                for j in range(4):
                    sub = ti * 4 + j
                    if sub + 2 < NSUB:
                        st_load(sub + 2)
                    if sub + 1 < NSUB and j < 3:
                        st_norm(sub + 1)
                    st_mm(sub)
                    if sub + 1 < NSUB and j < 3:
                        st_tp(sub + 1)
                    st_evac(sub)
                fs_ = fst[ti % 2]
                pp = ti % 2
                for cc in range(8):
                    pf = psF[cc % 2]
                    for k in range(KC):
                        mm(pf[:, :], w1fm[:, k, cc * 128:(cc + 1) * 128], xTt[:, k, :], k == 0, k == KC - 1,
                           [kxT + f".{j}.{k // 4}" for j in range(4)] + ["w1"], [f"psF{cc % 2}"])
                        if k % 4 == 3:
                            tick()
                    eng = "act" if cc % 2 == 0 else "dve"
                    if cc < 2:
                        cp(eng, qT2[pp][:, cc, :], pf[:, :], [f"psF{cc % 2}"], [f"qT{pp}.{cc}"])
                    elif cc < 4:
                        cp(eng, kT2[pp][:, cc - 2, :], pf[:, :], [f"psF{cc % 2}"], [f"kT{pp}.{cc - 2}"])
                    else:
                        cp(eng, fs_[:, cc - 4, :], pf[:, :], [f"psF{cc % 2}"], [f"fst{ti % 2}.{cc - 4}"])
                dma(qT_d[:, :, ti * 512:(ti + 1) * 512].rearrange("h p s -> p h s"), fs_[:, 0:2, :],
                    [f"fst{ti % 2}.0", f"fst{ti % 2}.1"], [])
                dma(kT_d[:, :, ti * 512:(ti + 1) * 512].rearrange("h p s -> p h s"), fs_[:, 2:4, :],
                    [f"fst{ti % 2}.2", f"fst{ti % 2}.3"], [])
                drain()
                state["gen"] = tile_chunks(ti)
                if ti + 1 < c.NT1:
                    st_norm((ti + 1) * 4)
                    st_tp((ti + 1) * 4)
            drain()
            dma(o_mc.rearrange("(k p) e -> p k e", p=128), cS[:, :, :], ["cS"], [])
            dma(o_mn, nS[:, :], ["nS"], [])
            dma(o_mm, m0[0:1, :], ["m0"], [])
            P.barrier()
            P.emit(block)
        if float(os.environ.get('KSTOP', '9')) <= 1:
            return nc

        with ExitStack() as st, nc.Block() as block:
            KTc = sb(st, "KTc", [128, 2, S], BF16)
            Vc = sb(st, "Vc", [128, 2, c.NB, 128], BF16)
            qtl = [sb(st, f"qtl{i}", [128, 2, 512], BF16) for i in range(2)]
            pt = [sb(st, f"pt{i}", [128, 512], BF16) for i in range(4)]
            bias_t = [sb(st, f"biast{i}", [128, 2, c.NB], F32) for i in range(2)]
            rD = sb(st, "rD", [128, 512], F32)
            hbT = [sb(st, f"hbT{i}", [128, 512], BF16) for i in range(2)]
            psS = [ps(st, f"psS{i}", [128, 512], F32) for i in range(3)]
            psO = [ps(st, f"psO{i}", [128, 512], F32) for i in range(2)]
            psDn = [ps(st, f"psDn{i}", [128, 512], F32) for i in range(2)]
            NCH = 8 if c.NB >= 8 else c.NB
            bpc = c.NB // NCH
            for q_ in range(NCH):
                s0, s1 = q_ * bpc * 128, (q_ + 1) * bpc * 128
                dma(KTc[:, :, s0:s1], kT_d[:, :, s0:s1].rearrange("h p s -> p h s"), [], [f"KTc{q_}"])
                dma(Vc[:, :, q_ * bpc:(q_ + 1) * bpc, :], v_d[:, :, q_ * bpc:(q_ + 1) * bpc, :].rearrange("h p b d -> p h b d"),
                    ["v_d"], [f"Vc{q_}"])
            xbv = xbuf.rearrange("(k p) s -> p k s", p=128)
            items = []
            for ti in range(c.NT1):
                for h in range(2):
                    for kb in range(4 * ti + 4):
                        items.append((ti, h, kb))
            LA = 2

            def stage_a(i):
                ti, h, kb = items[i]
                ql = qtl[ti % 2]
                bt = bias_t[ti % 2]
                nkb = 4 * ti + 4
                if h == 0 and kb == 0:
                    dma(ql[:, :, :], qT_d[:, :, ti * 512:(ti + 1) * 512].rearrange("h p s -> p h s"), [], [f"qtl{ti % 2}"])
                    for hh in range(2):
                        ts("dve", bt[:, hh, 0:nkb], Pk[:, hh, 0:nkb], Pend[:, hh, ti:ti + 1], None, ALU.subtract, None,
                           ["Pk", "Pend"], [f"biast{ti % 2}"])
                n0 = max(0, kb - 4 * ti) * 128
                pS = psS[i % 3]
                pb = pt[i % 4]
                kq = kb // bpc
                mm(pS[:, n0:512], KTc[:, h, kb * 128:(kb + 1) * 128], ql[:, h, n0:512], True, True,
                   [f"KTc{kq}", f"qtl{ti % 2}"], [f"psS{i % 3}"])
                act(pb[:, n0:512], pS[:, n0:512], AF.Exp, [f"psS{i % 3}", f"biast{ti % 2}"], [f"pt{i % 4}"],
                    bias=bt[:, h, kb:kb + 1], scale=SCALE)
                if kb >= 4 * ti:
                    tt("pool", pb[:, n0:n0 + 128], pb[:, n0:n0 + 128], triB, ALU.mult, [f"pt{i % 4}", "cB"], [f"pt{i % 4}"])

            def stage_b(i):
                ti, h, kb = items[i]
                nkb = 4 * ti + 4
                n0 = max(0, kb - 4 * ti) * 128
                pO, pD = psO[h], psDn[h]
                pb = pt[i % 4]
                kq = kb // bpc
                mm(pO[:, n0:512], Vc[:, h, kb, :], pb[:, n0:512], kb == 0, kb == nkb - 1,
                   [f"Vc{kq}", f"pt{i % 4}"], [f"psO{h}"])
                mm(pD[:, n0:512], onesB, pb[:, n0:512], kb == 0, kb == nkb - 1, ["cB", f"pt{i % 4}"], [f"psDn{h}"])
                if kb == nkb - 1:
                    recip(rD[:, :], pD[:, :], [f"psDn{h}"], ["rD"])
                    ob = hbT[(ti * 2 + h) % 2]
                    tt("dve", ob[:, :], pO[:, :], rD[:, :], ALU.mult, [f"psO{h}", "rD"], [f"hbT{(ti * 2 + h) % 2}"])
                    dma(xbv[:, 2 + h, ti * 512:(ti + 1) * 512], ob[:, :], [f"hbT{(ti * 2 + h) % 2}"], [f"xbuf{ti}.{h}"])

            for i in range(min(LA, len(items))):
                stage_a(i)
            for i in range(len(items)):
                if i + LA < len(items):
                    stage_a(i + LA)
                stage_b(i)
            A("pool", lambda e: e.collective_compute("AllGather", ALU.bypass, replica_groups=[list(range(8))],
                                                     ins=[cat], outs=[G]),
              [f"xbuf{ti}.{h}" for ti in range(c.NT1) for h in range(2)], [], cc=True)
            P.barrier()
            P.emit(block)
        if float(os.environ.get('KSTOP', '9')) <= 2:
            return nc

        NQ = c.DEC
        with ExitStack() as st, nc.Block() as block:
            grep = sb(st, "grepS", [128, D], F32)
            dma(grep[:NQ, :], nmix.partition_broadcast(NQ), [], ["grep"])
            xa_s = sb(st, "xa_s", [128, D], F32)
            junk = sb(st, "junkS", [128, D], BF16)
            xs_s = sb(st, "xs_s", [128, D], BF16)
            xT_s = sb(st, "xT_s", [128, KC, NQ], BF16)
            rs = sb(st, "rsS", [128, 8], F32)
            slab = [sb(st, f"slabS{i}", [128, KC, 512], BF16) for i in range(3)]
            mqT = sb(st, "mqT", [128, 8, NQ], BF16)
            mkT = sb(st, "mkT", [128, 8, NQ], BF16)
            fqT = sb(st, "fqT", [128, 8, NQ], BF16)
            fkT = sb(st, "fkT", [128, 8, NQ], BF16)
            ktok = sb(st, "ktokS", [128, 4, 256], BF16)
            vext = sb(st, "vextS", [128, 4, 257], BF16)
            sigo = sb(st, "sigoS", [128, 4, 256], F32)
            frow = [sb(st, f"frow{i}", [128, FW], F32) for i in range(2)]
            vnb = sb(st, "vnb", [128, FW], BF16)
            g16 = sb(st, "g16", [128, 16], F32)
            b16 = sb(st, "b16", [128, 16], F32)
            dma(b16[:NQ, :], bias_s.partition_broadcast(NQ), [], ["b16"])
            nh8 = sb(st, "nh8", [128, 8], F32)
            dma(nh8[:], normh_s, [], ["nh8"])
            e12 = sb(st, "e12", [128, 12], F32)
            sp12 = sb(st, "sp12", [128, 12], F32)
            sm = sb(st, "smS", [128, 4, 16], F32)
            m0a = sb(st, "m0a", [128, 4], F32)
            cS4 = sb(st, "cS4", [128, 4, 2, 256], F32)
            nS4 = sb(st, "nS4", [128, 8], F32)
            cbf = sb(st, "cbfS", [128, 2, 257], BF16)
            diag = sb(st, "diagS", [128, 128], F32)
            stT = [sb(st, f"stTS{i}", [128, 128], BF16) for i in range(2)]
            ku = [sb(st, f"kuS{i}", [128, 256], BF16) for i in range(2)]
            hout = [sb(st, f"houtS{i}", [128, 256], BF16) for i in range(2)]
            hsT = sb(st, "hsT", [128, 16, NQ], BF16)
            kb_p = sb(st, "kb_p", [128, 2, FW], BF16)
            Vp = sb(st, "Vp", [128, c.PB, FW], BF16)
            KTp = sb(st, "KTp", [128, FH, c.PAST], BF16)
            lfp = sb(st, "lfp", [128, c.PB, FH], F32)
            Pkp = sb(st, "Pkp", [128, FH, c.PB + 1], F32)
            carr8 = sb(st, "carr8", [128, FH], F32)
            sc_s = sb(st, "sc_s", [128, c.PB, NQ], F32)
            pts = sb(st, "pts", [128, c.PB + 1, NQ], BF16)
            rDs = sb(st, "rDs", [128, FH, NQ], F32)

            psT = ps(st, "psTS", [128, 8, 128], BF16)
            psR = [ps(st, f"psR{i}", [128, 512], F32) for i in range(2)]
            psF = [ps(st, f"psFS{i}", [128, 512], F32) for i in range(2)]
            psM = ps(st, "psMS", [128, 512], F32)
            psK = ps(st, "psKS", [128, 2, 256], F32)
            psX = ps(st, "psXS", [128, 512], F32)

            dma(xa_s[:NQ, :], x_s, [], ["xa"])
            act(junk[:NQ, :], xa_s[:NQ, :], AF.Square, ["xa"], ["junk", "rs"], accum=rs[:NQ, 0:1])
            act(rs[:NQ, 1:2], rs[:NQ, 0:1], AF.Ln, ["rs"], ["rs"], bias=epsT[:NQ, :], scale=1.0 / D)
            act(rs[:NQ, 2:3], rs[:NQ, 1:2], AF.Exp, ["rs"], ["rs"], scale=-0.5)
            stt(xs_s[:NQ, :], xa_s[:NQ, :], rs[:NQ, 2:3], grep[:NQ, :], ALU.mult, ALU.mult, ["xa", "rs", "grep"], ["xs"])
            for k0 in range(0, KC, 8):
                nk = min(8, KC - k0)
                for k in range(nk):
                    tp(psT[:, k, 0:NQ], xs_s[:NQ, (k0 + k) * 128:(k0 + k + 1) * 128], identB[:NQ, :NQ], ["xs", "cB"], ["psT"])
                cp("dve", xT_s[:, k0:k0 + nk, :], psT[:, 0:nk, 0:NQ], ["psT"], ["xT"])
            ckv = ck_s.rearrange("(b p) f -> p b f", p=128)
            cvv = cv_s.rearrange("(b p) f -> p b f", p=128)
            for b in range(c.PB):
                dma(Vp[:, b, :], cvv[:, b, :], [], [f"Vp{b}"], q="pool")
            dma(lfp[:, :, :], clf_s.rearrange("(b p) h -> p b h", p=128), [], ["lfp"])
            dma(cS4[:, :, :, :], mc_s.rearrange("h (k p) e -> p h k e", p=128), [], ["cS4"])
            dma(nS4[:, :], mn_s, [], ["nS4"])
            dma(m0a[:, :], mm_s.partition_broadcast(128), [], ["m0a"])
            for b in range(c.PB):
                dma(kb_p[:, b % 2, :], ckv[:, b, :], [], [f"kbp{b % 2}"], q="pool")
                for h in range(FH):
                    tp(psT[:, h, :], kb_p[:, b % 2, h * 128:(h + 1) * 128], identB, [f"kbp{b % 2}", "cB"], ["psT"])
                cp("act" if b % 2 == 0 else "dve", KTp[:, :, b * 128:(b + 1) * 128], psT[:, :, :], ["psT"], ["KTp"])
            ri = 0
            fi = 0
            for si in range(c.NSS):
                sl = slab[si % 3]
                ks = f"slabS{si % 3}"
                dma(sl[:, :, :], Wsl[si * 128:(si + 1) * 128, :].rearrange("p (k n) -> p k n", k=KC), [], [ks])
                grp = si // 2
                do_fm = grp in (0, 1, 4, 5)
                do_tm = grp in (1, 2, 3, 5, 6, 7)
                ncol = 16 if si == 14 else 512
                if do_tm:
                    pr = psR[ri % 2]
                    kr = f"psR{ri % 2}"
                    ri += 1
                    for k in range(KC):
                        mm(pr[:NQ, 0:ncol], xT_s[:, k, :], sl[:, k, 0:ncol], k == 0, k == KC - 1, ["xT", ks], [kr])
                    half = si % 2
                    if grp == 1:
                        for hh in range(2):
                            cp("act", ktok[:NQ, half * 2 + hh, :], pr[:NQ, hh * 256:(hh + 1) * 256], [kr], [f"ktok{half * 2 + hh}"])
                    elif grp == 2:
                        for hh in range(2):
                            cp("act", vext[:NQ, half * 2 + hh, 0:256], pr[:NQ, hh * 256:(hh + 1) * 256], [kr], [f"vext{half * 2 + hh}"])
                    elif grp == 3:
                        for hh in range(2):
                            hd = half * 2 + hh
                            act(sigo[:NQ, hd, :], pr[:NQ, hh * 256:(hh + 1) * 256], AF.Exp, [kr], [f"sigo{hd}"], scale=-1.0)
                            ts("pool", sigo[:NQ, hd, :], sigo[:NQ, hd, :], 1.0, None, ALU.add, None, [f"sigo{hd}"], [f"sigo{hd}"])
                            recip(sigo[:NQ, hd, :], sigo[:NQ, hd, :], [f"sigo{hd}"], [f"sigo{hd}"])
                    elif grp == 5:
                        cp("dve", frow[0][:NQ, half * 512:(half + 1) * 512], pr[:NQ, :], [kr], ["frow0"])
                    elif grp == 6:
                        cp("dve", frow[1][:NQ, half * 512:(half + 1) * 512], pr[:NQ, :], [kr], ["frow1"])
                    else:
                        tt("dve", g16[:NQ, :], pr[:NQ, 0:16], b16[:NQ, :], ALU.add, [kr, "b16"], ["g16"])
                if do_fm:
                    pf = psF[fi % 2]
                    kf = f"psFS{fi % 2}"
                    fi += 1
                    pfv = pf[:, 0:4 * NQ].rearrange("p (c n) -> p c n", c=4)
                    for cc in range(4):
                        for k in range(KC):
                            mm(pfv[:, cc, :], sl[:, k, cc * 128:(cc + 1) * 128], xT_s[:, k, :], k == 0, k == KC - 1, ["xT", ks], [kf])
                    dst = {0: mqT, 1: mkT, 4: fqT, 5: fkT}[grp]
                    half = si % 2
                    cp("act", dst[:, half * 4:(half + 1) * 4, :], pfv, [kf], [{0: "mqT", 1: "mkT", 4: "fqT", 5: "fkT"}[grp]])
            dma(o_sfk, frow[0][:NQ, :], ["frow0"], [])
            dma(o_sfv, frow[1][:NQ, :], ["frow1"], [])
            cp("pool", vnb[:NQ, :], frow[1][:NQ, :], ["frow1"], ["vnb"])
            for j in range(4):
                A("dve", lambda e, j=j: e.memset(vext[:, j, 256:257], 1.0), [], [f"vext{j}"])
            act(e12[:NQ, :], g16[:NQ, 4:16], AF.Exp, ["g16"], ["e12"], scale=-1.0)
            act(sp12[:NQ, :], e12[:NQ, :], AF.Ln, ["e12"], ["sp12"], bias=1.0)
            lfs = sb(st, "lfs", [128, FH], F32)
            ts("dve", lfs[:NQ, :], sp12[:NQ, 4:12], -1.0, None, ALU.mult, None, ["sp12"], ["lfs"])
            dma(o_sflf, lfs[:NQ, :], ["lfs"], [])
            cumN = psM[:NQ, 392:404]
            totN = psM[:, 404:416]
            mm(cumN, triF[:NQ, :NQ], sp12[:NQ, :], True, True, ["sp12", "cF"], ["psM"])
            mm(totN, onesF[:NQ, :], sp12[:NQ, :], True, True, ["sp12", "cF"], ["psM"])
            cum12 = sb(st, "cum12", [128, 12], F32)
            tot12 = sb(st, "tot12", [128, 12], F32)
            cp("dve", cum12[:NQ, :], cumN, ["psM"], ["cum12"])
            cp("dve", tot12[:, :], totN, ["psM"], ["tot12"])
            for h in range(MH):
                kj = f"{h}"
                smj = sm[:, h, :]
                L = NQ
                tt("dve", smj[:L, 0:1], g16[:L, h:h + 1], cum12[:L, h:h + 1], ALU.add, ["g16", "cum12"], ["sm" + kj])
                ts("dve", diag[:L, :L], identF[:L, :L], smj[:L, 0:1], None, ALU.mult, None, ["sm" + kj, "cF"], ["diag"])
                arow = psF[0][:, 0:L]
                mm(arow, onesF[:L, :], diag[:L, :L], True, True, ["diag", "cF"], ["psFS0"])
                A("dve", lambda e, smj=smj, arow=arow: e.tensor_reduce(out=smj[:, 1:2], in_=arow, axis=AX.X, op=ALU.max),
                  ["psFS0"], ["sm" + kj])
                tt("dve", smj[:, 2:3], smj[:, 1:2], m0a[:, h:h + 1], ALU.max, ["sm" + kj, "m0a"], ["sm" + kj])
                ts("dve", smj[:, 3:4], smj[:, 2:3], -1.0, None, ALU.mult, None, ["sm" + kj], ["sm" + kj])
                act(smj[:, 4:5], m0a[:, h:h + 1], AF.Exp, ["m0a", "sm" + kj], ["sm" + kj], bias=smj[:, 3:4])
                act(smj[:L, 5:6], smj[:L, 0:1], AF.Exp, ["sm" + kj], ["sm" + kj], bias=smj[:L, 3:4])
                ts("dve", smj[:L, 5:6], smj[:L, 5:6], MDK ** -0.5, None, ALU.mult, None, ["sm" + kj], ["sm" + kj])
                act(smj[:L, 6:7], cum12[:L, h:h + 1], AF.Exp, ["cum12", "sm" + kj], ["sm" + kj], bias=smj[:L, 3:4])
                mnew = sm[:, h, 13:14]
                tt("dve", mnew, smj[:, 2:3], tot12[:, h:h + 1], ALU.subtract, ["sm" + kj, "tot12"], ["sm" + kj])
                g0, u, flo = smj[:, 4:5], smj[:L, 5:6], smj[:L, 6:7]
                ST = psR[0][:L, 0:L]
                for dc in range(2):
                    mm(ST, mkT[:, h * 2 + dc, :], mqT[:, h * 2 + dc, :], dc == 0, dc == 1, ["mqT", "mkT"], ["psR0"])
                stb = stT[h % 2]
                stt(stb[:L, :L], ST, u, triB[:L, :L], ALU.mult, ALU.mult, ["psR0", "sm" + kj, "cB"], [f"stT{h % 2}"])
                kub = ku[h % 2]
                ts("pool", kub[:L, :], ktok[:L, h, :], u, None, ALU.mult, None, ["ktok" + kj, "sm" + kj], [f"ku{h % 2}"])
                act(cbf[:, :, 0:256], cS4[:, h, :, :], AF.Copy, ["cS4", "sm" + kj], ["cbf"], scale=g0)
                act(cbf[:, :, 256:257], nS4[:, h * 2:h * 2 + 2].unsqueeze(2), AF.Copy, ["nS4", "sm" + kj], ["cbf"], scale=g0)
                NE = psR[1][:L, 0:257]
                mm(NE, stb[:L, :L], vext[:L, h, :], True, False, [f"stT{h % 2}", "vext" + kj], ["psR1"])
                for dc in range(2):
                    mm(NE, mqT[:, h * 2 + dc, :], cbf[:, dc, :], False, dc == 1, ["mqT", "cbf"], ["psR1"])
                for dc in range(2):
                    mm(psK[:, dc, :], kub[:L, dc * 128:(dc + 1) * 128], vext[:L, h, 0:256], True, True,
                       [f"ku{h % 2}", "vext" + kj], ["psK"])
                    mm(psF[1][:, dc:dc + 1], kub[:L, dc * 128:(dc + 1) * 128], vext[:L, h, 256:257], True, True,
                       [f"ku{h % 2}", "vext" + kj], ["psFS1"])
                stt(cS4[:, h, :, :], cS4[:, h, :, :], g0, psK[:, :, :], ALU.mult, ALU.add, ["cS4", "psK", "sm" + kj, "cbf"], ["cS4"])
                stt(nS4[:, h * 2:h * 2 + 2], nS4[:, h * 2:h * 2 + 2], g0, psF[1][:, 0:2], ALU.mult, ALU.add,
                    ["nS4", "psFS1", "sm" + kj, "cbf"], ["nS4"])
                act(smj[:L, 14:15], NE[:, 256:257], AF.Abs, ["psR1"], ["sm" + kj])
                tt("dve", smj[:L, 7:8], smj[:L, 14:15], flo, ALU.max, ["sm" + kj], ["sm" + kj])
                recip(smj[:L, 8:9], smj[:L, 7:8], ["sm" + kj], ["sm" + kj])
                act(junk[:L, 0:256], NE[:, 0:256], AF.Square, ["psR1", "sm" + kj], ["junk", "sm" + kj],
                    scale=smj[:L, 8:9], accum=smj[:L, 9:10])
                act(smj[:L, 10:11], smj[:L, 9:10], AF.Ln, ["sm" + kj], ["sm" + kj], bias=epsT[:L, :], scale=1.0 / MDV)
                act(smj[:L, 11:12], smj[:L, 10:11], AF.Exp, ["sm" + kj], ["sm" + kj], scale=-0.5)
                tt("dve", smj[:L, 12:13], smj[:L, 11:12], smj[:L, 8:9], ALU.mult, ["sm" + kj], ["sm" + kj])
                hb_ = hout[h % 2]
                stt(hb_[:L, :], NE[:, 0:256], smj[:L, 12:13], sigo[:L, h, :], ALU.mult, ALU.mult,
                    ["psR1", "sm" + kj, "sigo" + kj], [f"hout{h % 2}"])
                for dc in range(2):
                    tp(psT[:, dc, 0:L], hb_[:L, dc * 128:(dc + 1) * 128], identB[:L, :L], [f"hout{h % 2}", "cB"], ["psT"])
                    ts("dve", hsT[:, h * 2 + dc, :], psT[:, dc, 0:L], nh8[:, h * 2 + dc:h * 2 + dc + 1], None, ALU.mult, None,
                       ["psT", "nh8"], ["hsT"])
            msout = sb(st, "msout", [128, MH], F32)
            for h in range(MH):
                cp("dve", msout[:, h:h + 1], sm[:, h, 13:14], [f"sm{h}"], ["msout"])
            dma(o_smc.rearrange("h (k p) e -> p h k e", p=128), cS4[:, :, :, :], ["cS4"], [])
            dma(o_smn, nS4[:, :], ["nS4"], [])
            dma(o_smm, msout[0:1, :], ["msout"], [])
            A("dve", lambda e: e.memset(carr8[:], 0.0), [], ["carr8"])
            A("dve", lambda e: e.memset(Pkp[:], 0.0), [], ["Pkp"])
            sp_p = sb(st, "sp_p", [128, c.PB, FH], F32)
            ts("dve", sp_p[:, :, :], lfp[:, :, :], -1.0, None, ALU.mult, None, ["lfp"], ["sp_p"])
            cumP = psR[0][:, 0:c.PB * FH]
            totP = psR[1][:, 0:c.PB * FH]
            spf = sp_p[:, :, :].rearrange("p b h -> p (b h)")
            mm(cumP, triF, spf, True, True, ["sp_p", "cF"], ["psR0"])
            mm(totP, onesF, spf, True, True, ["sp_p", "cF"], ["psR1"])
            for b in range(c.PB):
                tt("dve", Pkp[:, :, b], cumP[:, b * FH:(b + 1) * FH], carr8[:, :], ALU.add, ["psR0", "carr8"], ["Pkp"])
                tt("dve", carr8[:, :], carr8[:, :], totP[:, b * FH:(b + 1) * FH], ALU.add, ["psR1", "carr8"], ["carr8"])
            tt("dve", Pkp[:NQ, :, c.PB], cum12[:NQ, 4:12], carr8[:NQ, :], ALU.add, ["cum12", "carr8"], ["Pkp"])
            tt("dve", carr8[:, :], carr8[:, :], tot12[:, 4:12], ALU.add, ["tot12", "carr8"], ["carr8"])
            bias_k = sb(st, "bias_k", [128, FH, c.PB + 1], F32)
            tt("dve", bias_k[:, :, :], Pkp[:, :, :], carr8[:, :].unsqueeze(2).to_broadcast([128, FH, c.PB + 1]), ALU.subtract,
               ["Pkp", "carr8"], ["bias_k"])
            Oall = psX[:, 0:FH * NQ].rearrange("p (h n) -> p h n", h=FH)
            Dall = psK[:, :, :].rearrange("p a b -> p (a b)")[:, 0:FH * NQ].rearrange("p (h n) -> p h n", h=FH)
            for h in range(FH):
                pS = psR[h % 2]
                kS = f"psR{h % 2}"
                pSv = pS[:, 0:c.PB * NQ].rearrange("p (b n) -> p b n", b=c.PB)
                for b in range(c.PB):
                    mm(pSv[:, b, :], KTp[:, h, b * 128:(b + 1) * 128], fqT[:, h, :], True, True, ["KTp", "fqT"], [kS])
                stt(sc_s[:, :, :], pSv, SCALE, bias_k[:, h, 0:c.PB].unsqueeze(2).to_broadcast([128, c.PB, NQ]), ALU.mult, ALU.add,
                    [kS, "bias_k"], ["sc_s"])
                act(pts[:, 0:c.PB, :], sc_s[:, :, :], AF.Exp, ["sc_s"], ["pts"])
                pN = psM[:NQ, 0:NQ]
                mm(pN, fkT[:, h, :], fqT[:, h, :], True, True, ["fkT", "fqT"], ["psM"])
                act(pts[:NQ, c.PB, :], pN, AF.Exp, ["psM", "bias_k"], ["pts"], bias=bias_k[:NQ, h, c.PB:c.PB + 1], scale=SCALE)
                tt("pool", pts[:NQ, c.PB, :], pts[:NQ, c.PB, :], triB[:NQ, :NQ], ALU.mult, ["pts", "cB"], ["pts"])
                for b in range(c.PB):
                    mm(Oall[:, h, :], Vp[:, b, h * 128:(h + 1) * 128], pts[:, b, :], b == 0, False, [f"Vp{b}", "pts"], ["psX"])
                    mm(Dall[:, h, :], onesB, pts[:, b, :], b == 0, False, ["cB", "pts"], ["psK"])
                mm(Oall[:, h, :], vnb[:NQ, h * 128:(h + 1) * 128], pts[:NQ, c.PB, :], False, True, ["vnb", "pts"], ["psX"])
                mm(Dall[:, h, :], onesB[:NQ, :], pts[:NQ, c.PB, :], False, True, ["cB", "pts"], ["psK"])
            recip(rDs[:, :, :], Dall, ["psK"], ["rDs"])
            tt("dve", hsT[:, 8:16, :], Oall, rDs[:, :, :], ALU.mult, ["psX", "rDs"], ["hsT"])
            dma(hs_d, hsT[:, :, :], ["hsT"], [])
            P.barrier()
            P.emit(block)
        if float(os.environ.get('KSTOP', '9')) <= 3:
            return nc

        with ExitStack() as st, nc.Block() as block:
            xr = sb(st, "xr", [128, 4, D], F32)
            gv = sb(st, "gv", [128, D], F32)
            junk = sb(st, "junk4", [128, D], BF16)
            xs = [sb(st, f"xs4_{i}", [128, D], BF16) for i in range(2)]
            xT = sb(st, "xT4", [128, KC, 512], BF16)
            big = sb(st, "big", [128, max(c.FCH, 16 + KC), 512], BF16)
            slab = [sb(st, f"slab{i}", [128, c.SLABW], BF16) for i in range(3)]
            tmp = [sb(st, f"tmp{i}", [128, 512], F32) for i in range(5)]
            rs = sb(st, "rs4", [128, 8], F32)
            psT = ps(st, "psT4", [128, 8, 128], BF16)
            psX = [ps(st, f"psX{i}", [128, 512], F32) for i in range(4)]
            psY = [ps(st, f"psY{i}", [128, 512], F32) for i in range(2)]
            hT = lambda k: big[:, k, :]
            mT = lambda m: big[:, 16 + m, :]

            xg5 = Gv[:, NW:NW + 512, :].rearrange("(g r) (k p) s -> g p r k s", g=2, k=4)
            ctr = {"sl": 0}

            def load_slab(idx, width):
                i = ctr["sl"] % 3
                ctr["sl"] += 1
                dma(slab[i][:, 0:width], W2v(idx)[:, 0:width], [], [f"slab{i}"])
                return slab[i], f"slab{i}"

            def norm_t(NT, gsrc, kg, ydst=None):
                dma(gv[:, :], gsrc.partition_broadcast(128), [], ["gv"])
                nsub = (NT + 127) // 128
                for ts_ in range(nsub):
                    rows = min(128, NT - ts_ * 128)
                    xin = xr[:rows, ts_, :]
                    act(junk[:rows, :], xin, AF.Square, [f"xr{ts_}"], ["junk", "rs"], accum=rs[:rows, 0:1])
                    act(rs[:rows, 1:2], rs[:rows, 0:1], AF.Ln, ["rs"], ["rs"], bias=epsT[:rows, :], scale=1.0 / D)
                    act(rs[:rows, 2:3], rs[:rows, 1:2], AF.Exp, ["rs"], ["rs"], scale=-0.5)
                    if ydst is not None:
                        stt(xin, xin, rs[:rows, 2:3], gv[:rows, :], ALU.mult, ALU.mult, [f"xr{ts_}", "rs", "gv"], [f"xr{ts_}"])
                        dma(ydst[ts_ * 128:ts_ * 128 + rows, :], xin, [f"xr{ts_}"], [])
                        continue
                    xb = xs[ts_ % 2]
                    stt(xb[:rows, :], xin, rs[:rows, 2:3], gv[:rows, :], ALU.mult, ALU.mult, [f"xr{ts_}", "rs", "gv"], [f"xs{ts_ % 2}"])
                    for k0 in range(0, KC, 8):
                        nk = min(8, KC - k0)
                        for k in range(nk):
                            tp(psT[:, k, 0:rows], xb[:rows, (k0 + k) * 128:(k0 + k + 1) * 128], identB[:rows, :rows],
                               [f"xs{ts_ % 2}", "cB"], ["psT"])
                        cp("act" if (k0 // 8) % 2 == 0 else "dve", xT[:, k0:k0 + nk, ts_ * 128:ts_ * 128 + rows], psT[:, 0:nk, 0:rows],
                           ["psT"], ["xT"])

            def token_phase(NT, xsrc, ydst, prompt_tile):
                nsub = (NT + 127) // 128
                if prompt_tile is None:
                    dma(xr[:NT, 0, :], xsrc, [], ["xr0"])
                    for r in range(4):
                        dma(big[:, 4 * r:4 * r + 2, 0:NT], hs_d[:, 2 * r:2 * r + 2, :], [], [f"big{4 * r}", f"big{4 * r + 1}"])
                        dma(big[:, 4 * r + 2:4 * r + 4, 0:NT], hs_d[:, 8 + 2 * r:10 + 2 * r, :], [], [f"big{4 * r + 2}", f"big{4 * r + 3}"])
                else:
                    t0 = prompt_tile * 512
                    for ts_ in range(4):
                        dma(xr[:, ts_, :], xsrc[t0 + ts_ * 128:t0 + (ts_ + 1) * 128, :], [], [f"xr{ts_}"])
                    qn = "pool" if prompt_tile < (c.NT2 + 1) // 2 else "act"
                    for r in range(4):
                        def issue(e, t0=t0, r=r, qn=qn):
                            if qn not in ctr:
                                pid = e.partition_id()
                                ctr[qn] = (e.compute_val(pid // 4), e.compute_val((pid % 4) * c.Q))
                            gsel, qoff = ctr[qn]
                            src = xg5[bass.ds(gsel, 1), :, r, :, bass.ds(qoff + t0, 512)]
                            return e.dma_start(out=big[:, 4 * r:4 * r + 4, :], in_=src)
                        A(qn, issue, [], [f"big{4 * r + k}" for k in range(4)], dma=True)
                norm_t(NT, nmix, "gmix")
                hkeys = [f"big{k}" for k in range(16)]
                for m in range(KC):
                    sl, ks = load_slab(c.oG + m, (2 * KC + 16) * 128)
                    slv = sl[:, 0:(2 * KC + 16) * 128].rearrange("p (j n) -> p j n", n=128)
                    for k in range(KC):
                        mm(psX[0][:, 0:NT], slv[:, k, :], xT[:, k, 0:NT], k == 0, k == KC - 1, [ks, "xT"], ["psX0"])
                    for k in range(KC):
                        mm(psX[1][:, 0:NT], slv[:, KC + k, :], xT[:, k, 0:NT], k == 0, k == KC - 1, [ks, "xT"], ["psX1"])
                    for k in range(8):
                        ca = 4 * (k // 2) + (k % 2)
                        mm(psX[2][:, 0:NT], slv[:, 2 * KC + k, :], hT(ca)[:, 0:NT], k == 0, k == 7, [ks, f"big{ca}"], ["psX2"])
                    for k in range(8):
                        cb = 4 * (k // 2) + 2 + (k % 2)
                        mm(psX[3][:, 0:NT], slv[:, 2 * KC + 8 + k, :], hT(cb)[:, 0:NT], k == 0, k == 7, [ks, f"big{cb}"], ["psX3"])
                    for i in range(2):
                        act(tmp[i][:, 0:NT], psX[i][:, 0:NT], AF.Exp, [f"psX{i}"], [f"tmp{i}"], scale=-1.0)
                        ts("pool", tmp[i][:, 0:NT], tmp[i][:, 0:NT], 1.0, None, ALU.add, None, [f"tmp{i}"], [f"tmp{i}"])
                        recip(tmp[i][:, 0:NT], tmp[i][:, 0:NT], [f"tmp{i}"], [f"tmp{i}"])
                        tt("dve", tmp[2 + i][:, 0:NT], psX[2 + i][:, 0:NT], tmp[i][:, 0:NT], ALU.mult, [f"psX{2 + i}", f"tmp{i}"], [f"tmp{2 + i}"])
                    tt("pool", mT(m)[:, 0:NT], tmp[2][:, 0:NT], tmp[3][:, 0:NT], ALU.add, ["tmp2", "tmp3"], [f"big{16 + m}"])
                yi = 0
                for g in range(NG):
                    sl, ks = load_slab(c.oO + g, KC * DG)
                    slv = sl[:, 0:KC * DG].rearrange("p (k n) -> p k n", k=KC)
                    for ts_ in range(nsub):
                        rows = min(128, NT - ts_ * 128)
                        py = psY[yi % 2]
                        for k in range(KC):
                            mm(py[:rows, 0:DG], mT(k)[:, ts_ * 128:ts_ * 128 + rows], slv[:, k, :], k == 0, k == KC - 1,
                               [f"big{16 + k}", ks], [f"psY{yi % 2}"])
                        tt("dve", xr[:rows, ts_, g * DG:(g + 1) * DG], xr[:rows, ts_, g * DG:(g + 1) * DG], py[:rows, 0:DG], ALU.add,
                           [f"psY{yi % 2}", f"xr{ts_}"], [f"xr{ts_}"])
                        yi += 1
                norm_t(NT, nffn, "gffn")
                for fg in range(c.NUG):
                    sl, ks = load_slab(c.oU + fg, KC * 512)
                    slv = sl[:, 0:KC * 512].rearrange("p (k n) -> p k n", k=KC)
                    for fc in range(4):
                        py = psY[yi % 2]
                        for k in range(KC):
                            mm(py[:, 0:NT], slv[:, k, fc * 128:(fc + 1) * 128], xT[:, k, 0:NT], k == 0, k == KC - 1, [ks, "xT"], [f"psY{yi % 2}"])
                        tb = tmp[yi % 2]
                        act(tb[:, 0:NT], py[:, 0:NT], AF.Relu, [f"psY{yi % 2}"], [f"tmp{yi % 2}"])
                        f = fg * 4 + fc
                        tt("pool" if yi % 2 == 0 else "dve", big[:, f, 0:NT], tb[:, 0:NT], tb[:, 0:NT], ALU.mult, [f"tmp{yi % 2}"], [f"big{f}"])
                        yi += 1
                for g in range(NG):
                    for fs in range(c.NFS):
                        sl, ks = load_slab(c.oD + g * c.NFS + fs, c.FS * DG)
                        slv = sl[:, 0:c.FS * DG].rearrange("p (k n) -> p k n", k=c.FS)
                        for ts_ in range(nsub):
                            rows = min(128, NT - ts_ * 128)
                            for fc in range(c.FS):
                                f = fs * c.FS + fc
                                mm(psX[ts_][:rows, 0:DG], big[:, f, ts_ * 128:ts_ * 128 + rows], slv[:, fc, :],
                                   fs == 0 and fc == 0, fs == c.NFS - 1 and fc == c.FS - 1, [f"big{f}", ks], [f"psX{ts_}"])
                    for ts_ in range(nsub):
                        rows = min(128, NT - ts_ * 128)
                        tt("dve", xr[:rows, ts_, g * DG:(g + 1) * DG], xr[:rows, ts_, g * DG:(g + 1) * DG], psX[ts_][:rows, 0:DG], ALU.add,
                           [f"psX{ts_}", f"xr{ts_}"], [f"xr{ts_}"])
                norm_t(NT, nfin, "gfin", ydst=ydst)

            for t2 in range(c.NT2):
                token_phase(512, x_q, y_q[t2 * 512:(t2 + 1) * 512, :], t2)
            token_phase(NQ, x_s, y_s, None)
            P.barrier()
            P.emit(block)
    return nc


def _consts():
    i = np.eye(128, dtype=np.float32)
    t = np.triu(np.ones((128, 128), dtype=np.float32))
    o = np.ones((128, 128), dtype=np.float32)
    return np.ascontiguousarray(np.concatenate([i, t, o], axis=1))


def _pk(a, kc):
    n = a.shape[1]
    return np.ascontiguousarray(a.reshape(kc, 128, n).transpose(1, 0, 2))


def build_slabs(cfg, w_in, w_a, w_b, w_out, w_up, w_down):
    c = cfg
    D, KC = c.D, c.KC
    W = np.zeros((8 * c.SPC, 128, c.SLABW), dtype=np.float32)
    oga, ogb = NMIX, NMIX + D
    for m in range(KC):
        parts = [_pk(w_in[:, oga + m * 128:oga + (m + 1) * 128], KC), _pk(w_in[:, ogb + m * 128:ogb + (m + 1) * 128], KC),
                 _pk(w_a[:, m * 128:(m + 1) * 128], 8), _pk(w_b[:, m * 128:(m + 1) * 128], 8)]
        blk = np.concatenate(parts, axis=1).reshape(128, -1)
        W[c.oG + m, :, :blk.shape[1]] = blk
    for g in range(c.NG):
        blk = _pk(w_out[:, g * c.DG:(g + 1) * c.DG], KC).reshape(128, -1)
        W[c.oO + g, :, :blk.shape[1]] = blk
    for fg in range(c.NUG):
        blk = _pk(w_up[:, fg * 512:(fg + 1) * 512], KC).reshape(128, -1)
        W[c.oU + fg, :, :blk.shape[1]] = blk
    for g in range(c.NG):
        for fs in range(c.NFS):
            rows = w_down[fs * c.FS * 128:(fs + 1) * c.FS * 128, g * c.DG:(g + 1) * c.DG]
            blk = _pk(rows, c.FS).reshape(128, -1)
            W[c.oD + g * c.NFS + fs, :, :blk.shape[1]] = blk
    o = np.cumsum([0, MW, MW, MW, MW, MH, MH, FW, FW, FW, FH])
    mq, mk, mv, mo, mi, mf, fq, fk, fv, ff = [w_in[:, o[i]:o[i + 1]] for i in range(10)]
    perm = np.concatenate([mq, mk, mv, mo, fq, fk, fv, mi, mf, ff], axis=1)
    Ws = np.zeros((c.NSS, 128, KC * 512), dtype=np.float32)
    for si in range(c.NSS):
        cols = perm[:, si * 512:(si + 1) * 512]
        blk = np.zeros((128, KC, 512), dtype=np.float32)
        blk[:, :, :cols.shape[1]] = _pk(cols, KC)
        Ws[si] = blk.reshape(128, -1)
    return W, Ws


def pack_inputs(cfg, x_prompt, x_sample, cache_fox_k, cache_fox_v, cache_fox_logf, state_mlstm_c, state_mlstm_n,
                state_mlstm_m, norm_mix, w_in, b_mlstm_i, b_mlstm_f, b_fox_f, norm_mlstm_h, w_branch_a,
                w_branch_b, w_out, norm_ffn, w_up, w_down, norm_final):
    c = cfg
    f = lambda a: np.asarray(a, dtype=np.float32)
    x_prompt, x_sample = f(x_prompt), f(x_sample)
    w_in0 = f(w_in)[0]
    W, Ws = build_slabs(c, w_in0, f(w_branch_a)[0], f(w_branch_b)[0], f(w_out)[0], f(w_up)[0], f(w_down)[0])
    o = np.cumsum([0, MW, MW, MW, MW, MH, MH, FW, FW, FW, FH])
    mq, mk, mv, mo, mi, mf, fq, fk, fv, ff = [w_in0[:, o[i]:o[i + 1]] for i in range(10)]
    cst = _consts()
    nh = f(norm_mlstm_h)[0]
    maps = []
    for core in range(8):
        b, r = core // 4, core % 4
        hA, hB = 2 * r, 2 * r + 1
        sl = lambda a, h, w: a[:, h * w:(h + 1) * w]
        w1 = np.concatenate([
            sl(mq, r, 256), sl(mk, r, 256), sl(fq, hA, 128), sl(fq, hB, 128), sl(fk, hA, 128), sl(fk, hB, 128),
            sl(mk, r, 256), sl(mv, r, 256),
            sl(mo, r, 256), mi[:, r:r + 1], mf[:, r:r + 1], ff[:, hA:hA + 1], ff[:, hB:hB + 1],
            sl(fk, hA, 128), sl(fv, hA, 128), sl(fk, hB, 128), sl(fv, hB, 128)], axis=1)
        bias4 = np.array([[f(b_mlstm_i)[0, r], f(b_mlstm_f)[0, r], f(b_fox_f)[0, hA], f(b_fox_f)[0, hB]]], dtype=np.float32)
        bias_s = np.concatenate([f(b_mlstm_i)[0], f(b_mlstm_f)[0], f(b_fox_f)[0]])[None, :]
        maps.append({
            "x_b": np.ascontiguousarray(x_prompt[b]),
            "x_q": np.ascontiguousarray(x_prompt[b, r * c.Q:(r + 1) * c.Q]),
            "x_s": np.ascontiguousarray(x_sample[core]),
            "w1": np.ascontiguousarray(w1),
            "w2s": np.ascontiguousarray(W[core * c.SPC:(core + 1) * c.SPC].reshape(c.SPC * 128, c.SLABW)),
            "wss": Ws.reshape(c.NSS * 128, c.KC * 512),
            "bias4": bias4,
            "bias_s": np.ascontiguousarray(bias_s.astype(np.float32)),
            "nmix": f(norm_mix)[0][None, :], "nffn": f(norm_ffn)[0][None, :], "nfin": f(norm_final)[None, :],
            "normh_c": np.ascontiguousarray(nh[r * 256:(r + 1) * 256].reshape(2, 128).T),
            "normh_s": np.ascontiguousarray(nh.reshape(8, 128).T),
            "ck_s": np.ascontiguousarray(f(cache_fox_k)[0, core].reshape(c.PAST, FW)),
            "cv_s": np.ascontiguousarray(f(cache_fox_v)[0, core].reshape(c.PAST, FW)),
            "clf_s": np.ascontiguousarray(f(cache_fox_logf)[0, core]),
            "mc_s": np.ascontiguousarray(f(state_mlstm_c)[0, core]),
            "mn_s": np.ascontiguousarray(f(state_mlstm_n)[0, core].reshape(MH * 2, 128).T),
            "mm_s": np.ascontiguousarray(f(state_mlstm_m)[0, core][None, :]),
            "cst_f": cst,
        })
    return maps


def unpack_outputs(cfg, res):
    c = cfg
    B = 2
    y_prompt = np.zeros((B, c.S, c.D), np.float32)
    y_sample = np.zeros((8, c.DEC, c.D), np.float32)
    pk = np.zeros((1, B, c.S, FH, FD), np.float32)
    pv = np.zeros((1, B, c.S, FH, FD), np.float32)
    plf = np.zeros((1, B, c.S, FH), np.float32)
    pc = np.zeros((1, B, MH, MDK, MDV), np.float32)
    pn = np.zeros((1, B, MH, MDK), np.float32)
    pm = np.zeros((1, B, MH), np.float32)
    sk = np.zeros((1, 8, c.DEC, FH, FD), np.float32)
    sv = np.zeros((1, 8, c.DEC, FH, FD), np.float32)
    slf = np.zeros((1, 8, c.DEC, FH), np.float32)
    sc = np.zeros((1, 8, MH, MDK, MDV), np.float32)
    sn = np.zeros((1, 8, MH, MDK), np.float32)
    smm = np.zeros((1, 8, MH), np.float32)
    for core in range(8):
        o = res[core]
        b, r = core // 4, core % 4
        y_prompt[b, r * c.Q:(r + 1) * c.Q] = o["y_q"]
        y_sample[core] = o["y_s"]
        pk[0, b, :, 2 * r:2 * r + 2, :] = np.asarray(o["o_fk"]).reshape(c.S, 2, FD)
        pv[0, b, :, 2 * r:2 * r + 2, :] = np.asarray(o["o_fv"]).reshape(c.S, 2, FD)
        plf[0, b, :, 2 * r:2 * r + 2] = o["o_flf"]
        pc[0, b, r] = o["o_mc"]
        pn[0, b, r] = np.asarray(o["o_mn"]).T.reshape(MDK)
        pm[0, b, r] = np.asarray(o["o_mm"]).reshape(())
        sk[0, core] = np.asarray(o["o_sfk"]).reshape(c.DEC, FH, FD)
        sv[0, core] = np.asarray(o["o_sfv"]).reshape(c.DEC, FH, FD)
        slf[0, core] = o["o_sflf"]
        sc[0, core] = o["o_smc"]
        sn[0, core] = np.asarray(o["o_smn"]).T.reshape(MH, MDK)
        smm[0, core] = np.asarray(o["o_smm"]).reshape(MH)
    return (y_prompt, y_sample, pk, pv, plf, pc, pn, pm, sk, sv, slf, sc, sn, smm)


def kernel(**inputs):
    cfg = FULL
    maps = pack_inputs(cfg, **inputs)
    nc = build_program(cfg)
    res = run_bass_kernel_spmd(nc, maps, core_ids=list(range(8)))
    return unpack_outputs(cfg, res.results)
```
